# Optimizing a Trainium2 kernel written in Bass

```python
import jax
import jax.numpy as jnp
from jax import lax
import numpy as np

D_MODEL = 2048
BATCH = 8
SEQ = 2048
DEPTH = 2

GRID_W = 64
CTX_LEN = 256
D_CONV = D_MODEL // 4
D_MLSTM = D_MODEL // 2
D_SHORT = D_MODEL // 4
MLSTM_HEADS = 8
MLSTM_HEAD_DIM = D_MLSTM // MLSTM_HEADS
MLSTM_CHUNK = 64
CONV_A_WIDTH = 31
CONV_C_WIDTH = 3
D_FF = 4 * D_MODEL
N_GATES = 4 * MLSTM_HEADS
D_IN = 2 * D_CONV + 4 * D_MLSTM + N_GATES + 3 * D_SHORT
SPLIT_POINTS = (
    D_CONV,
    2 * D_CONV,
    2 * D_CONV + D_MLSTM,
    2 * D_CONV + 2 * D_MLSTM,
    2 * D_CONV + 3 * D_MLSTM,
    2 * D_CONV + 4 * D_MLSTM,
    2 * D_CONV + 4 * D_MLSTM + N_GATES,
    2 * D_CONV + 4 * D_MLSTM + N_GATES + D_SHORT,
    2 * D_CONV + 4 * D_MLSTM + N_GATES + 2 * D_SHORT,
)
NORM_EPS = 1e-6

kernel_name = "hybrid_conformer_mlstm_shortconv_dit_block"


def _rms_norm(x, w):
    xf = x.astype(jnp.float32)
    y = xf * lax.rsqrt(jnp.mean(xf * xf, axis=-1, keepdims=True) + NORM_EPS)
    return (y * w.astype(jnp.float32)).astype(x.dtype)


def _layer_norm(x, w, b):
    xf = x.astype(jnp.float32)
    mu = jnp.mean(xf, axis=-1, keepdims=True)
    var = jnp.mean(jnp.square(xf - mu), axis=-1, keepdims=True)
    y = (xf - mu) * lax.rsqrt(var + NORM_EPS)
    return (y * w.astype(jnp.float32) + b.astype(jnp.float32)).astype(x.dtype)


def _depthwise_conv(x, w, b):
    pad = w.shape[0] // 2
    y = lax.conv_general_dilated(
        x, w[:, None, :].astype(x.dtype), window_strides=(1,), padding=[(pad, pad)],
        dimension_numbers=("NWC", "WIO", "NWC"), feature_group_count=x.shape[-1])
    return y + b


def _short_conv_axis1(x, w):
    k = w.shape[0]
    pad = k // 2
    n = x.shape[1]
    widths = [(0, 0)] * x.ndim
    widths[1] = (pad, pad)
    p = jnp.pad(x, widths)
    y = w[0] * lax.slice_in_dim(p, 0, n, axis=1)
    for tap in range(1, k):
        y = y + w[tap] * lax.slice_in_dim(p, tap, tap + n, axis=1)
    return y


def _conformer_conv(a_val, a_gate, conv_w, conv_b, ln_w, ln_b, rows):
    u = a_val * jax.nn.sigmoid(a_gate)
    bsz, n, ch = u.shape
    if rows is None:
        y = _depthwise_conv(u, conv_w, conv_b)
    else:
        y = _depthwise_conv(u.reshape(bsz * rows, GRID_W, ch), conv_w, conv_b).reshape(bsz, n, ch)
    return jax.nn.silu(_layer_norm(y, ln_w, ln_b))


def _gated_short_conv(s_in, s_b, s_c, conv_w, rows):
    u = s_c * s_in
    bsz, n, ch = u.shape
    if rows is None:
        y = _short_conv_axis1(u, conv_w)
    else:
        y = _short_conv_axis1(u.reshape(bsz, rows, GRID_W, ch), conv_w).reshape(bsz, n, ch)
    return s_b * y


def _mlstm_chunked(q, k, v, log_i, log_f, state):
    bsz, heads, t_len, dh = q.shape
    n_chunks = t_len // MLSTM_CHUNK

    def to_chunks(a):
        a = a.reshape(bsz, heads, n_chunks, MLSTM_CHUNK, *a.shape[3:])
        return jnp.moveaxis(a, 2, 0)

    mask = jnp.tril(jnp.ones((MLSTM_CHUNK, MLSTM_CHUNK), dtype=bool))

    def step(carry, inp):
        c_prev, n_prev, m_prev = carry
        qc, kc, vc, lic, lfc = inp
        b = jnp.cumsum(lfc, axis=-1)
        log_d = jnp.where(mask, b[..., :, None] - b[..., None, :] + lic[..., None, :], -jnp.inf)
        log_inter = b + m_prev[..., None]
        m_row = jnp.maximum(log_inter, jnp.max(log_d, axis=-1))
        w_intra = jnp.exp(log_d - m_row[..., None])
        w_inter = jnp.exp(log_inter - m_row)
        scores = jnp.einsum("bhjd,bhsd->bhjs", qc, kc) * w_intra
        num = (jnp.einsum("bhjs,bhse->bhje", scores, vc)
               + w_inter[..., None] * jnp.einsum("bhjd,bhde->bhje", qc, c_prev))
        den = jnp.sum(scores, axis=-1) + w_inter * jnp.einsum("bhjd,bhd->bhj", qc, n_prev)
        h = num / jnp.maximum(jnp.abs(den), jnp.exp(-m_row))[..., None]
        b_last = b[..., -1]
        log_w = b_last[..., None] - b + lic
        m_new = jnp.maximum(b_last + m_prev, jnp.max(log_w, axis=-1))
        w_tok = jnp.exp(log_w - m_new[..., None])
        decay = jnp.exp(b_last + m_prev - m_new)
        kw = kc * w_tok[..., None]
        c_new = decay[..., None, None] * c_prev + jnp.einsum("bhsd,bhse->bhde", kw, vc)
        n_new = decay[..., None] * n_prev + jnp.sum(kw, axis=2)
        return (c_new, n_new, m_new), h

    state, h = lax.scan(step, state, (to_chunks(q), to_chunks(k), to_chunks(v),
                                      to_chunks(log_i), to_chunks(log_f)))
    return jnp.moveaxis(h, 0, 2).reshape(bsz, heads, t_len, dh), state


def _heads(t):
    bsz, n, _ = t.shape
    return t.reshape(bsz, n, MLSTM_HEADS, MLSTM_HEAD_DIM).transpose(0, 2, 1, 3).astype(jnp.float32)


def _mlstm_prep(q, k, v, g, gate_b):
    g = (g.astype(jnp.float32) + gate_b.astype(jnp.float32)).transpose(0, 2, 1)
    i_f, f_f, i_b, f_b = jnp.split(g, 4, axis=1)
    return (_heads(q), _heads(k) * MLSTM_HEAD_DIM ** -0.5, _heads(v),
            i_f, jax.nn.log_sigmoid(f_f), i_b, jax.nn.log_sigmoid(f_b))


def _mlstm_bidirectional(lat, ctx_in):
    qx, kx, vx, ixf, fxf, ixb, fxb = lat
    qc, kc, vc, icf, fcf, icb, fcb = ctx_in
    bsz = qx.shape[0]
    zero = (jnp.zeros((bsz, MLSTM_HEADS, MLSTM_HEAD_DIM, MLSTM_HEAD_DIM), jnp.float32),
            jnp.zeros((bsz, MLSTM_HEADS, MLSTM_HEAD_DIM), jnp.float32),
            jnp.zeros((bsz, MLSTM_HEADS), jnp.float32))
    hc_f, st_f = _mlstm_chunked(qc, kc, vc, icf, fcf, zero)
    hx_f, _ = _mlstm_chunked(qx, kx, vx, ixf, fxf, st_f)
    rev = lambda t: jnp.flip(t, axis=2)
    hc_b, st_b = _mlstm_chunked(rev(qc), rev(kc), rev(vc), rev(icb), rev(fcb), zero)
    hx_b, _ = _mlstm_chunked(rev(qx), rev(kx), rev(vx), rev(ixb), rev(fxb), st_b)
    return hx_f + rev(hx_b), hc_f + rev(hc_b)


def _mlstm_out(h, o, norm_w):
    h = h.transpose(0, 2, 1, 3)
    mu = jnp.mean(h, axis=-1, keepdims=True)
    var = jnp.mean(jnp.square(h - mu), axis=-1, keepdims=True)
    h = (h - mu) * lax.rsqrt(var + NORM_EPS)
    bsz, n = h.shape[:2]
    h = h.reshape(bsz, n, D_MLSTM) * norm_w.astype(jnp.float32)
    return (h * jax.nn.sigmoid(o.astype(jnp.float32))).astype(o.dtype)


def _sq_relu_mlp(h, w_ff1, w_ff2):
    return jnp.square(jax.nn.relu(h @ w_ff1)) @ w_ff2


def _layer(x, ctx, c, c_ctx, w_ada, b_ada, g_pre_mix, g_post_mix, g_pre_ffn, g_post_ffn,
           w_in, b_gates, conv_a_w, conv_a_b, ln_a_w, ln_a_b, mlstm_norm_w, conv_c_w,
           w_out, w_ff1, w_ff2, rows, update_ctx):
    mod_x = (jax.nn.silu(c) @ w_ada + b_ada)[:, None, :]
    mod_c = (jax.nn.silu(c_ctx) @ w_ada + b_ada)[None, None, :]
    sh1x, sc1x, g1x, sh2x, sc2x, g2x = jnp.split(mod_x, 6, axis=-1)
    sh1c, sc1c, g1c, sh2c, sc2c, g2c = jnp.split(mod_c, 6, axis=-1)

    hx = _rms_norm(x, g_pre_mix) * (1 + sc1x) + sh1x
    hc = _rms_norm(ctx, g_pre_mix) * (1 + sc1c) + sh1c
    px = jnp.split(hx @ w_in, SPLIT_POINTS, axis=-1)
    pc = jnp.split(hc @ w_in, SPLIT_POINTS, axis=-1)

    m_x, m_c = _mlstm_bidirectional(_mlstm_prep(px[2], px[3], px[4], px[6], b_gates),
                                    _mlstm_prep(pc[2], pc[3], pc[4], pc[6], b_gates))
    mix_x = jnp.concatenate([
        _conformer_conv(px[0], px[1], conv_a_w, conv_a_b, ln_a_w, ln_a_b, rows),
        _mlstm_out(m_x, px[5], mlstm_norm_w),
        _gated_short_conv(px[7], px[8], px[9], conv_c_w, rows),
    ], axis=-1) @ w_out
    x = x + g1x * _rms_norm(mix_x, g_post_mix)
    hx2 = _rms_norm(x, g_pre_ffn) * (1 + sc2x) + sh2x
    x = x + g2x * _rms_norm(_sq_relu_mlp(hx2, w_ff1, w_ff2), g_post_ffn)

    if update_ctx:
        mix_c = jnp.concatenate([
            _conformer_conv(pc[0], pc[1], conv_a_w, conv_a_b, ln_a_w, ln_a_b, None),
            _mlstm_out(m_c, pc[5], mlstm_norm_w),
            _gated_short_conv(pc[7], pc[8], pc[9], conv_c_w, None),
        ], axis=-1) @ w_out
        ctx = ctx + g1c * _rms_norm(mix_c, g_post_mix)
        hc2 = _rms_norm(ctx, g_pre_ffn) * (1 + sc2c) + sh2c
        ctx = ctx + g2c * _rms_norm(_sq_relu_mlp(hc2, w_ff1, w_ff2), g_post_ffn)
    return x, ctx


def setup_inputs(seed: int = 0) -> dict:
    key = jax.random.key(seed)
    ks = jax.random.split(key, 21)

    def nrm(k, shape, scale):
        return jax.random.normal(k, shape, jnp.float32) * scale

    kg = jax.random.split(ks[11], 4)
    fgate_init = jnp.linspace(3.0, 6.0, MLSTM_HEADS, dtype=jnp.float32)[None, :]
    b_gates = jnp.concatenate([
        nrm(kg[0], (DEPTH, MLSTM_HEADS), 0.1),
        fgate_init + nrm(kg[1], (DEPTH, MLSTM_HEADS), 0.1),
        nrm(kg[2], (DEPTH, MLSTM_HEADS), 0.1),
        fgate_init + nrm(kg[3], (DEPTH, MLSTM_HEADS), 0.1),
    ], axis=-1)
    return {
        "x": nrm(ks[0], (BATCH, SEQ, D_MODEL), 1.0),
        "c": nrm(ks[1], (BATCH, D_MODEL), 1.0),
        "ctx": nrm(ks[2], (BATCH, CTX_LEN, D_MODEL), 1.0),
        "c_ctx": nrm(ks[3], (D_MODEL,), 1.0),
        "w_ada": nrm(ks[4], (DEPTH, D_MODEL, 6 * D_MODEL), 0.5 * D_MODEL ** -0.5),
        "b_ada": nrm(ks[5], (DEPTH, 6 * D_MODEL), 0.02),
        "g_pre_mix": 1.0 + nrm(ks[6], (DEPTH, D_MODEL), 0.02),
        "g_post_mix": 1.0 + nrm(ks[7], (DEPTH, D_MODEL), 0.02),
        "g_pre_ffn": 1.0 + nrm(ks[8], (DEPTH, D_MODEL), 0.02),
        "g_post_ffn": 1.0 + nrm(ks[9], (DEPTH, D_MODEL), 0.02),
        "w_in": nrm(ks[10], (DEPTH, D_MODEL, D_IN), D_MODEL ** -0.5),
        "b_gates": b_gates,
        "conv_a_w": nrm(ks[12], (DEPTH, CONV_A_WIDTH, D_CONV), CONV_A_WIDTH ** -0.5),
        "conv_a_b": nrm(ks[13], (DEPTH, D_CONV), 0.02),
        "ln_a_w": 1.0 + nrm(ks[14], (DEPTH, D_CONV), 0.02),
        "ln_a_b": nrm(ks[15], (DEPTH, D_CONV), 0.02),
        "mlstm_norm_w": 1.0 + nrm(ks[16], (DEPTH, D_MLSTM), 0.02),
        "conv_c_w": nrm(ks[17], (DEPTH, CONV_C_WIDTH, D_SHORT), CONV_C_WIDTH ** -0.5),
        "w_out": nrm(ks[18], (DEPTH, D_MODEL, D_MODEL), D_MODEL ** -0.5),
        "w_ff1": nrm(ks[19], (DEPTH, D_MODEL, D_FF), D_MODEL ** -0.5),
        "w_ff2": nrm(ks[20], (DEPTH, D_FF, D_MODEL), D_FF ** -0.5),
    }


def reference(x, c, ctx, c_ctx, w_ada, b_ada, g_pre_mix, g_post_mix, g_pre_ffn, g_post_ffn,
              w_in, b_gates, conv_a_w, conv_a_b, ln_a_w, ln_a_b, mlstm_norm_w, conv_c_w,
              w_out, w_ff1, w_ff2):
    rows = x.shape[1] // GRID_W
    for layer in range(DEPTH):
        x, ctx = _layer(x, ctx, c, c_ctx, w_ada[layer], b_ada[layer], g_pre_mix[layer],
                        g_post_mix[layer], g_pre_ffn[layer], g_post_ffn[layer], w_in[layer],
                        b_gates[layer], conv_a_w[layer], conv_a_b[layer], ln_a_w[layer],
                        ln_a_b[layer], mlstm_norm_w[layer], conv_c_w[layer], w_out[layer],
                        w_ff1[layer], w_ff2[layer], rows=rows, update_ctx=layer < DEPTH - 1)
    return x
```

```python
import contextlib
import numpy as np
import concourse.bass as bass
import concourse.mybir as mybir
from concourse.bass_utils import run_bass_kernel_spmd

F32 = mybir.dt.float32
BF16 = mybir.dt.bfloat16
AF = mybir.ActivationFunctionType
ALU = mybir.AluOpType

L = 2
D = 2048
T = 2304
TC = 256
EPS = 1e-6
NV = 348
CUT = 0
USE_WB = False
TBS = [(0, 256), (256, 512), (768, 512), (1280, 512), (1792, 512)]
V_GPM, V_GQM, V_GPF, V_GQF, V_BADA, V_CAW, V_CAB, V_LNW, V_LNB, V_MNW, V_CCW, V_BG = 0, 16, 32, 48, 64, 160, 284, 288, 292, 296, 304, 316
C_ONE, C_MF, C_MB, C_NF, C_NB, C_ID = 0, 128, 256, 384, 512, 640


class Buf:
    def __init__(self, name, dsem=None):
        self.name = name
        self.w = {}
        self.r = {}
        self.dsem = dsem


class EngS:
    def __init__(self, name, h, sem, same_sync):
        self.name = name
        self.h = h
        self.sem = sem
        self.count = 0
        self.waited = {}
        self.same_sync = same_sync


class Sync:
    def __init__(self, nc, es, n_dma_sems=80):
        self.nc = nc
        mk = lambda n: es.enter_context(nc.semaphore(n))
        self.pe = EngS("pe", nc.tensor, mk("s_pe"), False)
        self.act = EngS("act", nc.scalar, mk("s_act"), True)
        self.dve = EngS("dve", nc.vector, mk("s_dve"), True)
        self.pool = EngS("pool", nc.gpsimd, mk("s_pool"), True)
        self.sp = EngS("sp", nc.sync, mk("s_sp"), False)
        self.engs = [self.pe, self.act, self.dve, self.pool, self.sp]
        self.dsems = [[mk(f"s_d{i}"), 0] for i in range(n_dma_sems)]
        self.free_dsems = list(range(n_dma_sems))
        self.nbuf = 0

    def buf(self, name=None, dma=False):
        self.nbuf += 1
        b = Buf(name or f"b{self.nbuf}")
        if dma:
            b.dsem = self.free_dsems.pop(0)
        return b

    def release(self, bufs):
        for b in bufs:
            if b.dsem is not None:
                self.free_dsems.append(b.dsem)
                b.dsem = None

    def _wait(self, e, sem, val):
        if e.waited.get(sem.num, 0) >= val:
            return
        e.h.wait_ge(sem, val)
        e.waited[sem.num] = val

    def _deps(self, e, reads, writes):
        deps = {}

        def add(d, war=False):
            for k, (sem, val) in d.items():
                if k == e.sem.num and (war or not e.same_sync):
                    continue
                if deps.get(k, (None, 0))[1] < val:
                    deps[k] = (sem, val)

        for b in reads:
            add(b.w)
        for b in writes:
            add(b.w)
            add(b.r, war=True)
        for k, (sem, val) in deps.items():
            self._wait(e, sem, val)

    def _record(self, ev, reads, writes, partial):
        sem, val = ev
        for b in reads:
            b.r[sem.num] = ev
        for b in writes:
            if partial:
                b.w[sem.num] = ev
            else:
                b.w = {sem.num: ev}
                b.r = {}

    def op(self, e, fn, reads=(), writes=(), partial=False):
        self._deps(e, reads, writes)
        ins = fn()
        e.count += 1
        ins.then_inc(e.sem, 1)
        self._record((e.sem, e.count), reads, writes, partial)

    def dma(self, q, out, in_, sb, reads=(), writes=(), partial=False):
        self._deps(q, reads, writes)
        ds = self.dsems[sb.dsem]
        q.h.dma_start(out=out, in_=in_).then_inc(ds[0], 16)
        ds[1] += 16
        self._record((ds[0], ds[1]), reads, writes, partial)

    def barrier(self):
        evs = [(e.sem, e.count) for e in self.engs if e.count > 0]
        evs += [(s, v) for s, v in self.dsems if v > 0]
        for e in self.engs:
            for sem, val in evs:
                if sem.num == e.sem.num:
                    continue
                self._wait(e, sem, val)


def build(debug=False, stop_after=None):
    nc = bass.Bass("TRN2", target_bir_lowering=False)
    din = lambda name, shape, dt=F32: nc.dram_tensor(name, shape, dt, kind="ExternalInput").ap()
    skind = "ExternalOutput" if debug else "Internal"
    dsc = lambda name, shape, dt=F32: nc.dram_tensor(name, shape, dt, kind=skind).ap()
    xT = din("xT", [128, 16, T])
    cv = din("cv", [128, 16, 2])
    consts = din("consts", [128, 768])
    vecs = din("vecs", [L, 128, NV])
    wada = din("wada", [L, 96, 128, 16, 128])
    winf = din("winf", [L, 44, 128, 16, 128])
    wint = din("wint", [L, 4, 128, 16, 512])
    wing = din("wing", [L, 128, 16, 32])
    wout = din("wout", [L, 16, 128, 16, 128])
    wff1 = din("wff1", [L, 64, 128, 16, 128])
    wff2 = din("wff2", [L, 64, 128, 16, 128])
    outT = nc.dram_tensor("outT", [128, 16, 2048], F32, kind="ExternalOutput").ap()
    XS = dsc("XS", [128, 16, T])
    UA = dsc("UA", [128, 4, T])
    UC = dsc("UC", [128, 4, T])
    SBS = dsc("SBS", [128, 4, T])
    SO = dsc("SO", [128, 8, T])
    QT = dsc("QT", [128, 8, T], BF16)
    KT = dsc("KT", [128, 8, T], BF16)
    KM = dsc("KM", [T, 1024], BF16)
    VM = dsc("VM", [T, 1024], BF16)
    GT = dsc("GT", [T, 32])
    MIXT = dsc("MIXT", [128, 16, T], BF16)
    HS = dsc("HS", [128, 8, T])
    WB = nc.dram_tensor("WB", [144, 128, 16, 128], BF16, kind="Internal").ap()
    if debug:
        DMOD = dsc("DMOD", [L, 128, 96, 2])
        DHT = dsc("DHT", [128, 16, T], BF16)

    with contextlib.ExitStack() as es:
        S = Sync(nc, es)

        uid = {"n": 0}

        def sbt(name, shape, dt=F32, st=None):
            uid["n"] += 1
            return (st or es).enter_context(nc.sbuf_tensor(f"{name}_u{uid['n']}", shape, dt))

        P4 = [es.enter_context(nc.psum_tensor(f"pp{i}", [128, 1024], F32)) for i in range(4)]
        PS = [P4[i // 2][:, (i % 2) * 512:(i % 2 + 1) * 512] for i in range(8)]
        PSB = [S.buf(f"ps{i}") for i in range(8)]

        CONST = sbt("const", [128, 768])
        CONSTB = S.buf("const", dma=True)
        S.dma(S.sp, CONST[:], consts, CONSTB, writes=[CONSTB])
        ones32 = CONST[:, C_ONE:C_ONE + 128]
        onesb = sbt("onesb", [128, 128], BF16)
        ONESB = S.buf("onesb")
        S.op(S.dve, lambda: nc.vector.tensor_copy(out=onesb[:], in_=ones32), reads=[CONSTB], writes=[ONESB])
        VEC = [sbt(f"vec{l}", [128, NV]) for l in range(L)]
        VECB = S.buf("vec", dma=True)
        for l in range(L):
            S.dma(S.sp, VEC[l][:], vecs[l], VECB, writes=[VECB], partial=True)
        MOD = [sbt(f"mod{l}", [128, 96, 2]) for l in range(L)]
        MODB = [S.buf(f"mod{l}", dma=True) for l in range(L)]
        GSC1 = [sbt(f"gsc1{l}", [128, 16, 2]) for l in range(L)]
        GG1 = [sbt(f"gg1{l}", [128, 16, 2]) for l in range(L)]
        GSC2 = [sbt(f"gsc2{l}", [128, 16, 2]) for l in range(L)]
        GG2 = [sbt(f"gg2{l}", [128, 16, 2]) for l in range(L)]
        DERB = [S.buf(f"der{l}") for l in range(L)]

        NR = 8
        ring = [sbt(f"ring{i}", [128, 16, 128], BF16) for i in range(NR)]
        RB = [S.buf(f"ring{i}", dma=True) for i in range(NR)]
        WQ = []
        WBD = S.buf("WB_dram")
        WQ += [(wada[0, t], "cast", None) for t in range(32)]
        for l in range(L):
            WQ += [(winf[l, t], "cast", None) for t in range(44)]
            if l == 0:
                WQ += [(wada[0, t], "cast", None) for t in range(32, 96)] + [(wada[1, t], "cast", None) for t in range(96)]
            nblk = 5 if l == 0 else 4
            for b_ in range(nblk):
                srcs = [wout[l, t] for t in range(16)] + [wff1[l, t] for t in range(64)] + [wff2[l, t] for t in range(64)]
                for j, sap in enumerate(srcs):
                    if USE_WB:
                        WQ.append((sap, "cast_store", j) if b_ == 0 else (None, "bf16", j))
                    else:
                        WQ.append((sap, "cast", None))
        wstate = {"issued": 0, "cur": 0}

        def wget(k=None):
            cur = wstate["cur"]
            while wstate["issued"] < min(len(WQ), cur + NR):
                i = wstate["issued"]
                s = i % NR
                src, mode, j = WQ[i]
                if mode == "bf16":
                    S.dma(S.pool, ring[s][:], WB[j], RB[s], reads=[WBD], writes=[RB[s]])
                else:
                    S.dma(S.pool, ring[s][:], src, RB[s], writes=[RB[s]])
                    if mode == "cast_store":
                        S.dma(S.sp, WB[j], ring[s][:], RB[s], reads=[RB[s]], writes=[WBD], partial=True)
                wstate["issued"] += 1
            if k is None:
                wstate["cur"] += 1
                return cur % NR
            wstate["cur"] += k
            return [(cur + j) % NR for j in range(k)]

        psrot = {"i": 0}

        def nextps(lo=2, hi=8):
            i = lo + psrot["i"] % (hi - lo)
            psrot["i"] += 1
            return i

        def mm16(pv, s, rhs_of_kc, start=True, stop=True):
            def f():
                for kc in range(16):
                    ins = nc.tensor.matmul(pv, lhsT=ring[s][:, kc, :], rhs=rhs_of_kc(kc),
                                           start=(start and kc == 0), stop=(stop and kc == 15))
                return ins
            return f

        cvt = sbt("cvt", [128, 16, 2], F32)
        cvb = sbt("cvb", [128, 16, 2], BF16)
        CV = S.buf("cv", dma=True)
        CVB = S.buf("cvb")

        def mod_finish(l, psm_i, t_lo, t_hi):
            psm = PS[psm_i]
            S.op(S.dve, lambda: nc.vector.tensor_tensor(
                out=MOD[l][:, t_lo:t_hi, :], in0=psm[:, 2 * t_lo:2 * t_hi].rearrange("p (t w) -> p t w", w=2),
                in1=VEC[l][:, V_BADA + t_lo:V_BADA + t_hi].unsqueeze(2).broadcast_to([128, t_hi - t_lo, 2]), op=ALU.add),
                reads=[PSB[psm_i], VECB], writes=[MODB[l]], partial=True)

        def derived(l, which):
            bc = lambda off: VEC[l][:, off:off + 16].unsqueeze(2).broadcast_to([128, 16, 2])
            if which == 0:
                S.op(S.dve, lambda: nc.vector.scalar_tensor_tensor(out=GSC1[l][:], in0=MOD[l][:, 16:32, :], scalar=1.0, in1=bc(V_GPM), op0=ALU.add, op1=ALU.mult),
                     reads=[MODB[l], VECB], writes=[DERB[l]], partial=True)
            else:
                S.op(S.dve, lambda: nc.vector.tensor_tensor(out=GG1[l][:], in0=MOD[l][:, 32:48, :], in1=bc(V_GQM), op=ALU.mult),
                     reads=[MODB[l], VECB], writes=[DERB[l]], partial=True)
                S.op(S.dve, lambda: nc.vector.scalar_tensor_tensor(out=GSC2[l][:], in0=MOD[l][:, 64:80, :], scalar=1.0, in1=bc(V_GPF), op0=ALU.add, op1=ALU.mult),
                     reads=[MODB[l], VECB], writes=[DERB[l]], partial=True)
                S.op(S.dve, lambda: nc.vector.tensor_tensor(out=GG2[l][:], in0=MOD[l][:, 80:96, :], in1=bc(V_GQF), op=ALU.mult),
                     reads=[MODB[l], VECB], writes=[DERB[l]], partial=True)

        def adaln_tile(psm_i, t, first):
            s = wget()
            S.op(S.pe, mm16(PS[psm_i][:, 2 * t:2 * t + 2], s, lambda kc: cvb[:, kc, :]),
                 reads=[RB[s], CVB], writes=[PSB[psm_i]], partial=(not first))

        def phase_adaln():
            S.dma(S.sp, cvt[:], cv, CV, writes=[CV])
            S.op(S.act, lambda: nc.scalar.activation(out=cvb[:], in_=cvt[:], func=AF.Silu), reads=[CV], writes=[CVB])
            for t in range(32):
                adaln_tile(0, t, t == 0)
            mod_finish(0, 0, 0, 32)
            derived(0, 0)
            S.barrier()

        def adaln_rest():
            for t in range(32, 96):
                adaln_tile(6, t, t == 32)
                yield
            for t in range(96):
                adaln_tile(7, t, t == 0)
                yield
            mod_finish(0, 6, 32, 96)
            derived(0, 1)
            mod_finish(1, 7, 0, 96)
            derived(1, 0)
            derived(1, 1)
            if debug:
                for l in range(L):
                    S.dma(S.sp, DMOD[l], MOD[l][:], MODB[l], reads=[MODB[l]])

        def rstd_from(ps_i, n, rt, RT, rstd, RSTD, div):
            S.op(S.act, lambda: nc.scalar.activation(out=rt[:, :n], in_=PS[ps_i][:, :n], func=AF.Sqrt, bias=EPS, scale=1.0 / div),
                 reads=[PSB[ps_i]], writes=[RT])
            S.op(S.dve, lambda: nc.vector.reciprocal(out=rstd[:, :n], in_=rt[:, :n]), reads=[RT], writes=[RSTD])

        def phase_inproj(l):
            xsrc = xT if l == 0 else XS
            with contextlib.ExitStack() as st:
                hT = sbt("hT", [128, 16, T], BF16, st)
                HT = S.buf("hT", dma=True)
                xb = sbt("xb", [128, 16, 512], F32, st)
                XB = S.buf("xb", dma=True)
                sq = [sbt(f"sq{i}", [128, 512], F32, st) for i in range(2)]
                SQ = [S.buf(f"sq{i}") for i in range(2)]
                tmp = [sbt(f"tmp{i}", [128, 512], F32, st) for i in range(2)]
                TMP = [S.buf(f"tmp{i}") for i in range(2)]
                rt = sbt("rt", [128, 512], F32, st); RT = S.buf("rt")
                rstd = sbt("rstd", [128, 512], F32, st); RSTD = S.buf("rstd")
                def norm_block(bi, t0, n):
                        w = 0 if bi == 0 else 1
                        S.dma(S.sp, xb[:, :, :n], xsrc[:, :, t0:t0 + n], XB, writes=[XB])
                        for c in range(16):
                            k = c % 2
                            S.op(S.act, lambda: nc.scalar.activation(out=sq[k][:, :n], in_=xb[:, c, :n], func=AF.Square), reads=[XB], writes=[SQ[k]])
                            S.op(S.pe, lambda: nc.tensor.matmul(PS[0][:, :n], lhsT=ones32, rhs=sq[k][:, :n], start=(c == 0), stop=(c == 15)),
                                 reads=[SQ[k], CONSTB], writes=[PSB[0]], partial=(c > 0))
                        rstd_from(0, n, rt, RT, rstd, RSTD, D)
                        for c in range(16):
                            k = c % 2
                            S.op(S.dve, lambda: nc.vector.scalar_tensor_tensor(out=tmp[k][:, :n], in0=xb[:, c, :n], scalar=GSC1[l][:, c, w:w + 1], in1=rstd[:, :n], op0=ALU.mult, op1=ALU.mult),
                                 reads=[XB, RSTD, DERB[l]], writes=[TMP[k]])
                            S.op(S.act, lambda: nc.scalar.activation(out=hT[:, c, t0:t0 + n], in_=tmp[k][:, :n], func=AF.Identity, bias=MOD[l][:, c, w:w + 1], scale=1.0),
                                 reads=[TMP[k], MODB[l]], writes=[HT], partial=True)

                hold = sbt("hold", [128, T], F32, st); HOLD = S.buf("hold")
                s32 = [sbt(f"s32_{i}", [128, 512], F32, st) for i in range(2)]
                S32 = [S.buf(f"s32_{i}", dma=True) for i in range(2)]
                s16 = [sbt(f"s16_{i}", [128, 512], BF16, st) for i in range(3)]
                S16 = [S.buf(f"s16_{i}", dma=True) for i in range(3)]
                rot = {"a": 0, "b": 0}
                kinds = []
                for i in range(4):
                    kinds += [("ag", i), ("av", i)]
                kinds += [("q", i) for i in range(8)] + [("k", i) for i in range(8)] + [("o", i) for i in range(8)]
                for i in range(4):
                    kinds += [("sin", i), ("sc", i)]
                kinds += [("sb", i) for i in range(4)]
                def do_tile(kind, idx, s, bi, t0, n):
                    pb = nextps()
                    pv = PS[pb][:, :n]
                    S.op(S.pe, mm16(pv, s, lambda kc: hT[:, kc, t0:t0 + n]), reads=[RB[s], HT], writes=[PSB[pb]])
                    if kind in ("ag", "sin"):
                        fn = AF.Sigmoid if kind == "ag" else AF.Copy
                        S.op(S.act, lambda: nc.scalar.activation(out=hold[:, t0:t0 + n], in_=pv, func=fn), reads=[PSB[pb]], writes=[HOLD], partial=True)
                    elif kind in ("av", "sc"):
                        k = rot["a"] % 2; rot["a"] += 1
                        S.op(S.dve, lambda: nc.vector.tensor_tensor(out=s32[k][:, :n], in0=pv, in1=hold[:, t0:t0 + n], op=ALU.mult),
                             reads=[PSB[pb], HOLD], writes=[S32[k]])
                        dst = UA if kind == "av" else UC
                        S.dma(S.sp, dst[:, idx, t0:t0 + n], s32[k][:, :n], S32[k], reads=[S32[k]])
                    elif kind in ("q", "k"):
                        k = rot["b"] % 3; rot["b"] += 1
                        sc_ = 1.0 if kind == "q" else 128.0 ** -0.5
                        S.op(S.act, lambda: nc.scalar.mul(out=s16[k][:, :n], in_=pv, mul=sc_), reads=[PSB[pb]], writes=[S16[k]])
                        dst = QT if kind == "q" else KT
                        S.dma(S.sp, dst[:, idx, t0:t0 + n], s16[k][:, :n], S16[k], reads=[S16[k]])
                    else:
                        k = rot["a"] % 2; rot["a"] += 1
                        fn = AF.Sigmoid if kind == "o" else AF.Copy
                        S.op(S.act, lambda: nc.scalar.activation(out=s32[k][:, :n], in_=pv, func=fn), reads=[PSB[pb]], writes=[S32[k]])
                        dst = SO if kind == "o" else SBS
                        S.dma(S.sp, dst[:, idx, t0:t0 + n], s32[k][:, :n], S32[k], reads=[S32[k]])

                first8 = wget(8)
                for bi, (t0, n) in enumerate(TBS):
                    norm_block(bi, t0, n)
                    if l == 1 and bi == 0:
                        continue
                    for j_ in range(8):
                        do_tile(kinds[j_][0], kinds[j_][1], first8[j_], bi, t0, n)
                if debug and l == 0:
                    S.dma(S.sp, DHT, hT[:], HT, reads=[HT])
                for kind, idx in kinds[8:]:
                    s = wget()
                    for bi, (t0, n) in enumerate(TBS):
                        if l == 1 and bi == 0:
                            continue
                        do_tile(kind, idx, s, bi, t0, n)
                wide = [sbt(f"wide{i}", [128, 16, 512], BF16, st) for i in range(2)]
                WIDE = [S.buf(f"wide{i}", dma=True) for i in range(2)]
                gtile = sbt("gtile", [128, 16, 32], BF16, st); GTILE = S.buf("gtile", dma=True)
                sg = [sbt(f"sg{i}", [128, 32], F32, st) for i in range(2)]
                SG = [S.buf(f"sg{i}", dma=True) for i in range(2)]
                S.dma(S.pool, gtile[:], wing[l], GTILE, writes=[GTILE])
                for wi in range(4):
                    wb = wi % 2
                    for g4 in range(4):
                        S.dma(S.pool, wide[wb][:, 4 * g4:4 * g4 + 4, :], wint[l, wi, :, 4 * g4:4 * g4 + 4, :], WIDE[wb], writes=[WIDE[wb]], partial=(g4 > 0))
                    for tt in range(18):
                        pb = nextps()

                        def mmw(pb=pb, tt=tt, wb=wb):
                            for kc in range(16):
                                ins = nc.tensor.matmul(PS[pb][:, :], lhsT=hT[:, kc, tt * 128:(tt + 1) * 128], rhs=wide[wb][:, kc, :],
                                                       start=(kc == 0), stop=(kc == 15))
                            return ins
                        S.op(S.pe, mmw, reads=[WIDE[wb], HT], writes=[PSB[pb]])
                        k = rot["b"] % 3; rot["b"] += 1
                        sc_ = 1.0 if wi < 2 else 128.0 ** -0.5
                        S.op(S.act, lambda: nc.scalar.mul(out=s16[k][:, :], in_=PS[pb][:, :], mul=sc_), reads=[PSB[pb]], writes=[S16[k]])
                        dst = VM if wi < 2 else KM
                        S.dma(S.sp, dst[tt * 128:(tt + 1) * 128, wb * 512:(wb + 1) * 512], s16[k][:, :], S16[k], reads=[S16[k]])
                for tt in range(18):
                    pb = nextps()

                    def mmg(pb=pb, tt=tt):
                        for kc in range(16):
                            ins = nc.tensor.matmul(PS[pb][:, 0:32], lhsT=hT[:, kc, tt * 128:(tt + 1) * 128], rhs=gtile[:, kc, :],
                                                   start=(kc == 0), stop=(kc == 15))
                        return ins
                    S.op(S.pe, mmg, reads=[GTILE, HT], writes=[PSB[pb]])
                    k = tt % 2
                    S.op(S.dve, lambda: nc.vector.tensor_tensor(out=sg[k][:], in0=PS[pb][:, 0:32], in1=VEC[l][:, V_BG:V_BG + 32], op=ALU.add),
                         reads=[PSB[pb], VECB], writes=[SG[k]])
                    S.dma(S.sp, GT[tt * 128:(tt + 1) * 128, :], sg[k][:], SG[k], reads=[SG[k]])
                S.barrier()
                S.release([HT, XB, GTILE] + S32 + S16 + WIDE + SG)

        def phase_conv(l):
            blocks = TBS if l == 0 else TBS[1:]
            lo = 0 if l == 0 else TC
            with contextlib.ExitStack() as st:
                Y = sbt("Y", [128, 4, T], F32, st); YB = [S.buf(f"Y{i}") for i in range(4)]
                U = [sbt(f"U{i}", [128, T], F32, st) for i in range(2)]
                UB = [S.buf(f"U{i}", dma=True) for i in range(2)]
                Y2 = sbt("Y2", [128, T], F32, st); Y2B = S.buf("Y2")
                UPL = sbt("UPL", [128, 4, 32, 94], BF16, st); UPLB = [S.buf(f"UPL{i}") for i in range(4)]
                UPC = sbt("UPC", [128, 4, 286], BF16, st); UPCB = [S.buf(f"UPC{i}") for i in range(4)]
                DG = sbt("DG", [128, 4, 31, 128], BF16, st); DGB = [S.buf(f"DG{i}") for i in range(4)]
                ident = CONST[:, C_ID:C_ID + 128]
                agen = adaln_rest() if l == 0 else None
                pshi = 6 if l == 0 else 8

                def apump(k):
                    if agen is not None:
                        for _ in range(k):
                            next(agen, None)
                S.op(S.pool, lambda: nc.gpsimd.memset(UPL[:].rearrange("p a r j -> p (a r j)"), 0.0), writes=UPLB)
                if l == 0:
                    S.op(S.pool, lambda: nc.gpsimd.memset(UPC[:].rearrange("p a j -> p (a j)"), 0.0), writes=UPCB)
                for i in range(4):
                    u = U[i % 2]
                    UBi = UB[i % 2]
                    S.dma(S.sp, u[:, lo:], UA[:, i, lo:], UBi, writes=[UBi])

                    def dgop(i=i):
                        for k in range(31):
                            ins = nc.vector.tensor_scalar_mul(out=DG[:, i, k, :], in0=ident, scalar1=VEC[l][:, V_CAW + i * 31 + k:V_CAW + i * 31 + k + 1])
                        return ins
                    S.op(S.dve, dgop, reads=[CONSTB, VECB], writes=[DGB[i]])
                    S.op(S.act, lambda: nc.scalar.copy(out=UPL[:, i, :, 15:79], in_=u[:, TC:].rearrange("p (r j) -> p r j", j=64)),
                         reads=[UBi], writes=[UPLB[i]], partial=True)
                    if l == 0:
                        S.op(S.act, lambda: nc.scalar.copy(out=UPC[:, i, 15:271], in_=u[:, 0:TC]), reads=[UBi], writes=[UPCB[i]], partial=True)
                    bias = VEC[l][:, V_CAB + i:V_CAB + i + 1]
                    for rb in range(4):
                        pb = nextps(2, pshi)
                        pv = PS[pb].rearrange("p (r j) -> p r j", j=64)

                        def mmc(i=i, rb=rb, pv=pv):
                            for k in range(31):
                                ins = nc.tensor.matmul(pv, lhsT=DG[:, i, k, :], rhs=UPL[:, i, rb * 8:rb * 8 + 8, k:k + 64], start=(k == 0), stop=(k == 30))
                            return ins
                        S.op(S.pe, mmc, reads=[DGB[i], UPLB[i]], writes=[PSB[pb]])
                        t0_ = TC + rb * 512
                        S.op(S.act, lambda: nc.scalar.activation(out=Y[:, i, t0_:t0_ + 512], in_=PS[pb], func=AF.Identity, bias=bias, scale=1.0),
                             reads=[PSB[pb], VECB], writes=[YB[i]], partial=True)
                        apump(8)
                    if l == 0:
                        pb = nextps(2, pshi)

                        def mmcc(i=i, pb=pb):
                            for k in range(31):
                                ins = nc.tensor.matmul(PS[pb][:, 0:256], lhsT=DG[:, i, k, :], rhs=UPC[:, i, k:k + 256], start=(k == 0), stop=(k == 30))
                            return ins
                        S.op(S.pe, mmcc, reads=[DGB[i], UPCB[i]], writes=[PSB[pb]])
                        S.op(S.act, lambda: nc.scalar.activation(out=Y[:, i, 0:256], in_=PS[pb][:, 0:256], func=AF.Identity, bias=bias, scale=1.0),
                             reads=[PSB[pb], VECB], writes=[YB[i]], partial=True)
                apump(200)
                sq = [sbt(f"csq{i}", [128, 512], F32, st) for i in range(2)]
                SQ = [S.buf(f"csq{i}") for i in range(2)]
                mean = sbt("cmean", [128, 512], F32, st); MEAN = S.buf("cmean")
                msq = sbt("cmsq", [128, 512], F32, st); MSQ = S.buf("cmsq")
                var = sbt("cvar", [128, 512], F32, st); VAR = S.buf("cvar")
                rt = sbt("crt", [128, 512], F32, st); RT = S.buf("crt")
                rstd = sbt("crstd", [128, 512], F32, st); RSTD = S.buf("crstd")
                t1 = [sbt(f"ct1{i}", [128, 512], F32, st) for i in range(2)]
                T1 = [S.buf(f"ct1{i}") for i in range(2)]
                o16 = [sbt(f"co16{i}", [128, 512], BF16, st) for i in range(2)]
                O16 = [S.buf(f"co16{i}", dma=True) for i in range(2)]
                for (t0, n) in blocks:
                    for i in range(4):
                        k = i % 2
                        S.op(S.act, lambda: nc.scalar.activation(out=sq[k][:, :n], in_=Y[:, i, t0:t0 + n], func=AF.Square), reads=[YB[i]], writes=[SQ[k]])
                        S.op(S.pe, lambda: nc.tensor.matmul(PS[0][:, :n], lhsT=ones32, rhs=sq[k][:, :n], start=(i == 0), stop=(i == 3)),
                             reads=[SQ[k], CONSTB], writes=[PSB[0]], partial=(i > 0))
                        S.op(S.pe, lambda: nc.tensor.matmul(PS[1][:, :n], lhsT=ones32, rhs=Y[:, i, t0:t0 + n], start=(i == 0), stop=(i == 3)),
                             reads=[YB[i], CONSTB], writes=[PSB[1]], partial=(i > 0))
                    S.op(S.dve, lambda: nc.vector.tensor_scalar_mul(out=mean[:, :n], in0=PS[1][:, :n], scalar1=1.0 / 512), reads=[PSB[1]], writes=[MEAN])
                    S.op(S.dve, lambda: nc.vector.tensor_tensor(out=msq[:, :n], in0=mean[:, :n], in1=mean[:, :n], op=ALU.mult), reads=[MEAN], writes=[MSQ])
                    S.op(S.dve, lambda: nc.vector.scalar_tensor_tensor(out=var[:, :n], in0=PS[0][:, :n], scalar=1.0 / 512, in1=msq[:, :n], op0=ALU.mult, op1=ALU.subtract),
                         reads=[PSB[0], MSQ], writes=[VAR])
                    S.op(S.act, lambda: nc.scalar.activation(out=rt[:, :n], in_=var[:, :n], func=AF.Sqrt, bias=EPS, scale=1.0), reads=[VAR], writes=[RT])
                    S.op(S.dve, lambda: nc.vector.reciprocal(out=rstd[:, :n], in_=rt[:, :n]), reads=[RT], writes=[RSTD])
                    for i in range(4):
                        k = i % 2
                        S.op(S.dve, lambda: nc.vector.tensor_tensor(out=t1[k][:, :n], in0=Y[:, i, t0:t0 + n], in1=mean[:, :n], op=ALU.subtract), reads=[YB[i], MEAN], writes=[T1[k]])
                        S.op(S.dve, lambda: nc.vector.scalar_tensor_tensor(out=t1[k][:, :n], in0=t1[k][:, :n], scalar=VEC[l][:, V_LNW + i:V_LNW + i + 1], in1=rstd[:, :n], op0=ALU.mult, op1=ALU.mult),
                             reads=[T1[k], RSTD, VECB], writes=[T1[k]])
                        S.op(S.act, lambda: nc.scalar.activation(out=o16[k][:, :n], in_=t1[k][:, :n], func=AF.Silu, bias=VEC[l][:, V_LNB + i:V_LNB + i + 1], scale=1.0),
                             reads=[T1[k], VECB], writes=[O16[k]])
                        S.dma(S.sp, MIXT[:, i, t0:t0 + n], o16[k][:, :n], O16[k], reads=[O16[k]])
                SBV = sbt("sbv", [128, T], F32, st); SBVB = S.buf("sbv", dma=True)
                for i in range(4):
                    u = U[i % 2]
                    UBi = UB[i % 2]
                    S.dma(S.sp, u[:, lo:], UC[:, i, lo:], UBi, writes=[UBi])
                    S.dma(S.sp, SBV[:, lo:], SBS[:, i, lo:], SBVB, writes=[SBVB])
                    wc = lambda k: VEC[l][:, V_CCW + i * 3 + k:V_CCW + i * 3 + k + 1]
                    S.op(S.dve, lambda: nc.vector.tensor_scalar_mul(out=Y2[:, lo:], in0=u[:, lo:], scalar1=wc(1)), reads=[UBi, VECB], writes=[Y2B])
                    rngs = [(TC, T, 64)] + ([(0, TC, 1)] if l == 0 else [])
                    for (a, b, sh) in rngs:
                        S.op(S.dve, lambda: nc.vector.scalar_tensor_tensor(out=Y2[:, a + sh:b], in0=u[:, a:b - sh], scalar=wc(0), in1=Y2[:, a + sh:b], op0=ALU.mult, op1=ALU.add),
                             reads=[UBi, VECB, Y2B], writes=[Y2B], partial=True)
                        S.op(S.dve, lambda: nc.vector.scalar_tensor_tensor(out=Y2[:, a:b - sh], in0=u[:, a + sh:b], scalar=wc(2), in1=Y2[:, a:b - sh], op0=ALU.mult, op1=ALU.add),
                             reads=[UBi, VECB, Y2B], writes=[Y2B], partial=True)
                    for (t0, n) in blocks:
                        k = (t0 // 256) % 2
                        S.op(S.dve, lambda: nc.vector.tensor_tensor(out=o16[k][:, :n], in0=Y2[:, t0:t0 + n], in1=SBV[:, t0:t0 + n], op=ALU.mult),
                             reads=[Y2B, SBVB], writes=[O16[k]])
                        S.dma(S.sp, MIXT[:, 12 + i, t0:t0 + n], o16[k][:, :n], O16[k], reads=[O16[k]])
                S.barrier()
                S.release(UB + O16 + [SBVB])

        def phase_mlstm(l):
            with contextlib.ExitStack() as st:
                HB = [S.buf(f"H{c}") for c in range(18)]
                mk = lambda n, shape, dt: sbt(n, shape, dt, st)
                LD = ["qT", "kT", "km", "vm", "g"]
                X = []
                for d in range(2):
                    x = {}
                    for nm in ["PT", "qs", "vw", "Cb", "nB"]:
                        x[nm] = mk(f"{nm}{d}", [128, 8, 128], BF16)
                    for nm in ["qT", "kT"]:
                        x[nm] = [mk(f"{nm}{d}{j}", [128, 8, 128], BF16) for j in range(2)]
                    for nm in ["km", "vm"]:
                        x[nm] = [mk(f"{nm}{d}{j}", [128, 1024], BF16) for j in range(2)]
                    x["g"] = [mk(f"g{d}{j}", [128, 32], F32) for j in range(2)]
                    for nm in ["R", "EB", "ARG", "Dm", "C32", "dn", "hh"]:
                        x[nm] = mk(f"{nm}{d}", [128, 8, 128], F32)
                    for nm in ["e1", "lp", "lf", "c1", "a2", "wtok", "n32"]:
                        x[nm] = mk(f"{nm}{d}", [128, 8], F32)
                    x["wtb"] = mk(f"wtb{d}", [128, 8], BF16)
                    x["B"] = {nm: S.buf(f"{nm}{d}", dma=(nm == "hh"))
                              for nm in ["PT", "qs", "vw", "Cb", "nB", "R", "EB", "ARG", "Dm", "C32", "dn", "hh",
                                         "e1", "lp", "lf", "c1", "a2", "wtok", "n32", "wtb"]}
                    for nm in LD:
                        x["B"][nm] = [S.buf(f"{nm}{d}{j}", dma=True) for j in range(2)]
                    x["SOc"] = mk(f"SOc{d}", [128, 8, 128], F32); x["B"]["SOc"] = S.buf(f"SOc{d}", dma=True)
                    x["hprev"] = mk(f"hprev{d}", [128, 8, 128], F32); x["B"]["hprev"] = S.buf(f"hprev{d}", dma=True)
                    for nm in ["fmean", "cen", "fsq", "frt", "frs"]:
                        x[nm] = mk(f"{nm}{d}", [128, 4, 128], F32); x["B"][nm] = S.buf(f"{nm}{d}")
                    x["m16"] = mk(f"m16{d}", [128, 8, 128], BF16); x["B"]["m16"] = S.buf(f"m16{d}", dma=True)
                    X.append(x)

                fl = lambda ap: ap.rearrange("p h e -> p (h e)")
                p3 = lambda ap: ap.rearrange("p (h e) -> p h e", e=128)

                def need(ch):
                    return l == 0 or ch >= 2

                def loads(d, ch, par):
                    x = X[d]; B = x["B"]
                    tok0 = ch * 128
                    S.dma(S.sp, x["km"][par][:], KM[tok0:tok0 + 128, :], B["km"][par], writes=[B["km"][par]])
                    S.dma(S.sp, x["vm"][par][:], VM[tok0:tok0 + 128, :], B["vm"][par], writes=[B["vm"][par]])
                    S.dma(S.sp, x["g"][par][:], GT[tok0:tok0 + 128, :], B["g"][par], writes=[B["g"][par]])
                    if need(ch):
                        S.dma(S.sp, x["qT"][par][:], QT[:, :, tok0:tok0 + 128], B["qT"][par], writes=[B["qT"][par]])
                        S.dma(S.sp, x["kT"][par][:], KT[:, :, tok0:tok0 + 128], B["kT"][par], writes=[B["kT"][par]])

                visited = set()

                def unit(d, ch, par, first, last):
                    x = X[d]; B = x["B"]
                    need_out = need(ch)
                    tok0 = ch * 128
                    bk = 4 * d
                    iBL, iST, iNM, iSM = bk, bk + 1, bk + 2, bk + 3
                    BLD, STD, NUMF, SM = PS[iBL], PS[iST], PS[iNM], PS[iSM]
                    co_m, co_n, lastc = (C_MF, C_NF, 127) if d == 0 else (C_MB, C_NB, 0)
                    Mk = CONST[:, co_m:co_m + 128]
                    NEG = CONST[:, co_n:co_n + 128]
                    gb = 0 if d == 0 else 16
                    km, vm, g, qT, kT = x["km"][par], x["vm"][par], x["g"][par], x["qT"][par], x["kT"][par]
                    Bkm, Bvm, Bg, BqT, BkT = B["km"][par], B["vm"][par], B["g"][par], B["qT"][par], B["kT"][par]
                    S.op(S.act, lambda: nc.scalar.activation(out=x["e1"][:], in_=g[:, gb + 8:gb + 16], func=AF.Exp, scale=-1.0), reads=[Bg], writes=[B["e1"]])
                    S.op(S.act, lambda: nc.scalar.activation(out=x["lp"][:], in_=x["e1"][:], func=AF.Ln, bias=1.0, scale=1.0), reads=[B["e1"]], writes=[B["lp"]])
                    yield
                    S.op(S.dve, lambda: nc.vector.tensor_scalar_mul(out=x["lf"][:], in0=x["lp"][:], scalar1=-1.0), reads=[B["lp"]], writes=[B["lf"]])
                    S.op(S.dve, lambda: nc.vector.tensor_tensor(out=x["R"][:], in0=Mk.unsqueeze(1).broadcast_to([128, 8, 128]),
                                                                in1=x["lf"][:].unsqueeze(2).broadcast_to([128, 8, 128]), op=ALU.mult),
                         reads=[CONSTB, B["lf"]], writes=[B["R"]])
                    if need_out and not first:
                        pass
                    yield
                    for half in range(2):
                        hs = slice(4 * half, 4 * half + 4)

                        def mmbl(half=half, hs=hs):
                            ins = nc.tensor.matmul(BLD, lhsT=ones32, rhs=fl(x["R"][:, hs, :]), start=True, stop=True)
                            if half == 0:
                                ins = nc.tensor.matmul(SM[:, 0:8], lhsT=Mk, rhs=x["lf"][:], start=True, stop=True)
                            return ins
                        S.op(S.pe, mmbl, reads=[B["R"], B["lf"], CONSTB], writes=[PSB[iBL]] + ([PSB[iSM]] if half == 0 else []))
                        yield
                        S.op(S.act, lambda: nc.scalar.activation(out=fl(x["EB"][:, hs, :]), in_=BLD, func=AF.Exp), reads=[PSB[iBL]], writes=[B["EB"]], partial=True)
                        if half == 0:
                            S.op(S.dve, lambda: nc.vector.scalar_tensor_tensor(out=x["c1"][:], in0=SM[:, 0:8], scalar=-1.0, in1=g[:, gb:gb + 8], op0=ALU.mult, op1=ALU.add),
                                 reads=[Bg, PSB[iSM]], writes=[B["c1"]])
                        yield

                        def argop(half=half):
                            for hh_ in range(4):
                                ins = nc.vector.tensor_tensor(out=x["ARG"][:, 4 * half + hh_, :], in0=BLD[:, hh_ * 128:(hh_ + 1) * 128], in1=NEG, op=ALU.add)
                            return ins
                        S.op(S.dve, argop, reads=[PSB[iBL], CONSTB, B["EB"]], writes=[B["ARG"]], partial=True)
                        yield
                        if need_out:
                            def dmop(half=half):
                                for hh_ in range(4):
                                    h = 4 * half + hh_
                                    ins = nc.scalar.activation(out=x["Dm"][:, h, :], in_=x["ARG"][:, h, :], func=AF.Exp, bias=x["c1"][:, h:h + 1], scale=1.0)
                                return ins
                            S.op(S.act, dmop, reads=[B["ARG"], B["c1"]], writes=[B["Dm"]], partial=True)
                            yield
                    if need_out:
                        if not first:
                            S.op(S.dve, lambda: nc.vector.tensor_tensor(out=x["qs"][:], in0=qT[:], in1=x["EB"][:], op=ALU.mult), reads=[BqT, B["EB"]], writes=[B["qs"]])
                        for half in range(2):
                            hs = slice(4 * half, 4 * half + 4)

                            def mmst(half=half):
                                for hh_ in range(4):
                                    h = 4 * half + hh_
                                    ins = nc.tensor.matmul(STD[:, hh_ * 128:(hh_ + 1) * 128], lhsT=kT[:, h, :], rhs=qT[:, h, :], start=True, stop=True)
                                return ins
                            S.op(S.pe, mmst, reads=[BkT, BqT], writes=[PSB[iST]])
                            yield
                            S.op(S.dve, lambda: nc.vector.tensor_tensor(out=x["PT"][:, hs, :], in0=p3(STD), in1=x["Dm"][:, hs, :], op=ALU.mult),
                                 reads=[PSB[iST], B["Dm"]], writes=[B["PT"]], partial=True)
                            yield

                            def mmnum(half=half):
                                for hh_ in range(4):
                                    h = 4 * half + hh_
                                    cs = slice(hh_ * 128, (hh_ + 1) * 128)
                                    ins = nc.tensor.matmul(NUMF[:, cs], lhsT=vm[:, h * 128:(h + 1) * 128], rhs=x["PT"][:, h, :], start=True, stop=first)
                                    if not first:
                                        ins = nc.tensor.matmul(NUMF[:, cs], lhsT=x["Cb"][:, h, :], rhs=x["qs"][:, h, :], start=False, stop=True)
                                for hh_ in range(4):
                                    h = 4 * half + hh_
                                    cs = slice(hh_ * 128, (hh_ + 1) * 128)
                                    ins = nc.tensor.matmul(STD[:, cs], lhsT=onesb[:], rhs=x["PT"][:, h, :], start=True, stop=first)
                                    if not first:
                                        ins = nc.tensor.matmul(STD[:, cs], lhsT=x["nB"][:, h, :], rhs=x["qs"][:, h, :], start=False, stop=True)
                                return ins
                            rd = [Bvm, B["PT"], ONESB] + ([B["Cb"], B["nB"], B["qs"]] if not first else [])
                            S.op(S.pe, mmnum, reads=rd, writes=[PSB[iNM], PSB[iST]])
                            yield
                            S.op(S.act, lambda: nc.scalar.activation(out=fl(x["dn"][:, hs, :]), in_=STD, func=AF.Abs), reads=[PSB[iST]], writes=[B["dn"]], partial=True)
                            yield
                            S.op(S.dve, lambda: nc.vector.tensor_scalar_max(out=x["dn"][:, hs, :], in0=x["dn"][:, hs, :], scalar1=1.0), reads=[B["dn"]], writes=[B["dn"]], partial=True)
                            S.op(S.dve, lambda: nc.vector.reciprocal(out=x["dn"][:, hs, :], in_=x["dn"][:, hs, :]), reads=[B["dn"]], writes=[B["dn"]], partial=True)
                            S.op(S.dve, lambda: nc.vector.tensor_tensor(out=x["hh"][:, hs, :], in0=p3(NUMF), in1=x["dn"][:, hs, :], op=ALU.mult),
                                 reads=[PSB[iNM], B["dn"]], writes=[B["hh"]], partial=True)
                            yield
                        if ch not in visited:
                            S.dma(S.sp, HS[:, :, tok0:tok0 + 128], x["hh"][:], B["hh"], reads=[B["hh"]], writes=[HB[ch]])
                        else:
                            S.dma(S.sp, x["hprev"][:], HS[:, :, tok0:tok0 + 128], B["hprev"], reads=[HB[ch]], writes=[B["hprev"]])
                            S.dma(S.sp, x["SOc"][:], SO[:, :, tok0:tok0 + 128], B["SOc"], writes=[B["SOc"]])
                    if not last:
                        S.op(S.dve, lambda: nc.vector.tensor_tensor(out=x["a2"][:], in0=x["c1"][:], in1=x["ARG"][:, :, lastc], op=ALU.add), reads=[B["c1"], B["ARG"]], writes=[B["a2"]])
                        yield
                        S.op(S.act, lambda: nc.scalar.activation(out=x["wtok"][:], in_=x["a2"][:], func=AF.Exp), reads=[B["a2"]], writes=[B["wtok"]])
                        yield
                        S.op(S.dve, lambda: nc.vector.tensor_tensor(out=x["vw"][:], in0=p3(vm[:]), in1=x["wtok"][:].unsqueeze(2).broadcast_to([128, 8, 128]), op=ALU.mult),
                             reads=[Bvm, B["wtok"]], writes=[B["vw"]])
                        S.op(S.dve, lambda: nc.vector.tensor_copy(out=x["wtb"][:], in_=x["wtok"][:]), reads=[B["wtok"]], writes=[B["wtb"]])
                        dec = x["EB"][:, :, lastc]
                        if not first:
                            S.op(S.dve, lambda: nc.vector.tensor_tensor(out=x["C32"][:], in0=x["C32"][:], in1=dec.unsqueeze(2).broadcast_to([128, 8, 128]), op=ALU.mult),
                                 reads=[B["C32"], B["EB"]], writes=[B["C32"]])
                            S.op(S.dve, lambda: nc.vector.tensor_tensor(out=x["n32"][:], in0=x["n32"][:], in1=dec, op=ALU.mult), reads=[B["n32"], B["EB"]], writes=[B["n32"]])
                        yield
                        for half in range(2):
                            hs = slice(4 * half, 4 * half + 4)

                            def mmdel(half=half):
                                for hh_ in range(4):
                                    h = 4 * half + hh_
                                    ins = nc.tensor.matmul(BLD[:, hh_ * 128:(hh_ + 1) * 128], lhsT=km[:, h * 128:(h + 1) * 128], rhs=x["vw"][:, h, :], start=True, stop=True)
                                if half == 0:
                                    for h in range(8):
                                        ins = nc.tensor.matmul(SM[:, 8 + h:9 + h], lhsT=km[:, h * 128:(h + 1) * 128], rhs=x["wtb"][:, h:h + 1], start=True, stop=True)
                                return ins
                            S.op(S.pe, mmdel, reads=[Bkm, B["vw"], B["wtb"]], writes=[PSB[iBL]] + ([PSB[iSM]] if half == 0 else []))
                            yield
                            if first:
                                S.op(S.dve, lambda: nc.vector.tensor_copy(out=x["C32"][:, hs, :], in_=p3(BLD)), reads=[PSB[iBL]], writes=[B["C32"]], partial=True)
                            else:
                                S.op(S.dve, lambda: nc.vector.tensor_tensor(out=x["C32"][:, hs, :], in0=p3(BLD), in1=x["C32"][:, hs, :], op=ALU.add),
                                     reads=[PSB[iBL], B["C32"]], writes=[B["C32"]], partial=True)
                            if half == 0:
                                if first:
                                    S.op(S.dve, lambda: nc.vector.tensor_copy(out=x["n32"][:], in_=SM[:, 8:16]), reads=[PSB[iSM]], writes=[B["n32"]])
                                else:
                                    S.op(S.dve, lambda: nc.vector.tensor_tensor(out=x["n32"][:], in0=SM[:, 8:16], in1=x["n32"][:], op=ALU.add), reads=[B["n32"], PSB[iSM]], writes=[B["n32"]])
                            yield
                        S.op(S.act, lambda: nc.scalar.copy(out=x["Cb"][:], in_=x["C32"][:]), reads=[B["C32"]], writes=[B["Cb"]])
                        S.op(S.dve, lambda: nc.vector.tensor_copy(out=x["nB"][:], in_=x["n32"][:].unsqueeze(2).broadcast_to([128, 8, 128])), reads=[B["n32"]], writes=[B["nB"]])
                        yield
                    if need_out:
                        if ch in visited:
                            S.op(S.dve, lambda: nc.vector.tensor_tensor(out=x["hh"][:], in0=x["hh"][:], in1=x["hprev"][:], op=ALU.add), reads=[B["hh"], B["hprev"]], writes=[B["hh"]])
                            yield
                            for half in range(2):
                                hs = slice(4 * half, 4 * half + 4)
                                S.op(S.pe, lambda: nc.tensor.matmul(NUMF, lhsT=ones32, rhs=fl(x["hh"][:, hs, :]), start=True, stop=True),
                                     reads=[B["hh"], CONSTB], writes=[PSB[iNM]])
                                yield
                                S.op(S.dve, lambda: nc.vector.tensor_scalar_mul(out=fl(x["fmean"][:]), in0=NUMF, scalar1=1.0 / 128), reads=[PSB[iNM]], writes=[B["fmean"]])
                                S.op(S.dve, lambda: nc.vector.tensor_tensor(out=x["cen"][:], in0=x["hh"][:, hs, :], in1=x["fmean"][:], op=ALU.subtract), reads=[B["hh"], B["fmean"]], writes=[B["cen"]])
                                yield
                                S.op(S.act, lambda: nc.scalar.activation(out=x["fsq"][:], in_=x["cen"][:], func=AF.Square), reads=[B["cen"]], writes=[B["fsq"]])
                                yield
                                S.op(S.pe, lambda: nc.tensor.matmul(NUMF, lhsT=ones32, rhs=fl(x["fsq"][:]), start=True, stop=True),
                                     reads=[B["fsq"], CONSTB], writes=[PSB[iNM]])
                                yield
                                S.op(S.act, lambda: nc.scalar.activation(out=fl(x["frt"][:]), in_=NUMF, func=AF.Sqrt, bias=EPS, scale=1.0 / 128), reads=[PSB[iNM]], writes=[B["frt"]])
                                yield
                                S.op(S.dve, lambda: nc.vector.reciprocal(out=x["frs"][:], in_=x["frt"][:]), reads=[B["frt"]], writes=[B["frs"]])
                                S.op(S.dve, lambda: nc.vector.tensor_tensor(out=x["cen"][:], in0=x["cen"][:], in1=x["frs"][:], op=ALU.mult), reads=[B["cen"], B["frs"]], writes=[B["cen"]])

                                def m16op(half=half):
                                    for hh_ in range(4):
                                        h = 4 * half + hh_
                                        ins = nc.vector.scalar_tensor_tensor(out=x["m16"][:, h, :], in0=x["cen"][:, hh_, :], scalar=VEC[l][:, V_MNW + h:V_MNW + h + 1], in1=x["SOc"][:, h, :], op0=ALU.mult, op1=ALU.mult)
                                    return ins
                                S.op(S.dve, m16op, reads=[B["cen"], B["SOc"], VECB], writes=[B["m16"]], partial=True)
                                yield
                            S.dma(S.sp, MIXT[:, 4:12, tok0:tok0 + 128], x["m16"][:], B["m16"], reads=[B["m16"]])
                        visited.add(ch)

                Fo = list(range(18))
                Bo = [1, 0] + list(range(17, 1, -1))
                loads(0, Fo[0], 0)
                loads(1, Bo[0], 0)
                for i in range(18):
                    par = i % 2
                    if i + 1 < 18:
                        loads(0, Fo[i + 1], 1 - par)
                        loads(1, Bo[i + 1], 1 - par)
                    gens = [unit(0, Fo[i], par, i == 0, i == 17), unit(1, Bo[i], par, i == 0, i == 17)]
                    while gens:
                        for g_ in list(gens):
                            try:
                                next(g_)
                            except StopIteration:
                                gens.remove(g_)
                S.barrier()
                rel = []
                for d in range(2):
                    for v in X[d]["B"].values():
                        rel += v if isinstance(v, list) else [v]
                S.release(rel)

        def phase_outffn(l):
            xsrc = xT if l == 0 else XS
            with contextlib.ExitStack() as st:
                Xb = sbt("Xb", [128, 16, 512], F32, st); XBB = S.buf("Xb", dma=True)
                S1 = sbt("S1", [128, 16, 512], F32, st); S1B = [S.buf(f"S1_{c}") for c in range(16)]
                A16 = sbt("A16", [128, 16, 512], BF16, st); A16B = S.buf("A16", dma=True)
                HID = sbt("HID", [128, 64, 512], BF16, st); HIDB = [S.buf(f"HID{c}") for c in range(64)]
                RL = [sbt(f"RL{i}", [128, 512], F32, st) for i in range(2)]; RLB = [S.buf(f"RL{i}") for i in range(2)]
                sq = [sbt(f"osq{i}", [128, 512], F32, st) for i in range(2)]; SQ = [S.buf(f"osq{i}") for i in range(2)]
                tmp = [sbt(f"otmp{i}", [128, 512], F32, st) for i in range(2)]; TMP = [S.buf(f"otmp{i}") for i in range(2)]
                rt = sbt("ort", [128, 512], F32, st); RT = S.buf("ort")
                rstd = sbt("orstd", [128, 512], F32, st); RSTD = S.buf("orstd")
                cnt = {"sq": 0}

                def stat_add(src_ap, src_bufs, n, c, ncs):
                    k = cnt["sq"] % 2; cnt["sq"] += 1
                    S.op(S.act, lambda: nc.scalar.activation(out=sq[k][:, :n], in_=src_ap, func=AF.Square), reads=src_bufs, writes=[SQ[k]])
                    S.op(S.pe, lambda: nc.tensor.matmul(PS[0][:, :n], lhsT=ones32, rhs=sq[k][:, :n], start=(c == 0), stop=(c == ncs - 1)),
                         reads=[SQ[k], CONSTB], writes=[PSB[0]], partial=(c > 0))

                def resid(G, w, n):
                    for c in range(16):
                        k = c % 2
                        S.op(S.dve, lambda: nc.vector.scalar_tensor_tensor(out=tmp[k][:, :n], in0=S1[:, c, :n], scalar=G[l][:, c, w:w + 1], in1=rstd[:, :n], op0=ALU.mult, op1=ALU.mult),
                             reads=[S1B[c], RSTD, DERB[l]], writes=[TMP[k]])
                        S.op(S.dve, lambda: nc.vector.tensor_tensor(out=Xb[:, c, :n], in0=Xb[:, c, :n], in1=tmp[k][:, :n], op=ALU.add),
                             reads=[TMP[k], XBB], writes=[XBB], partial=True)

                for bi, (t0, n) in enumerate(TBS):
                    if l == 1 and bi == 0:
                        continue
                    w = 0 if bi == 0 else 1
                    S.dma(S.sp, A16[:, :, :n], MIXT[:, :, t0:t0 + n], A16B, writes=[A16B])
                    S.dma(S.sp, Xb[:, :, :n], xsrc[:, :, t0:t0 + n], XBB, writes=[XBB])
                    for fb in range(16):
                        s = wget()
                        pb = nextps()
                        pv = PS[pb][:, :n]
                        S.op(S.pe, mm16(pv, s, lambda kc: A16[:, kc, :n]), reads=[RB[s], A16B], writes=[PSB[pb]])
                        S.op(S.act, lambda: nc.scalar.copy(out=S1[:, fb, :n], in_=pv), reads=[PSB[pb]], writes=[S1B[fb]])
                        stat_add(S1[:, fb, :n], [S1B[fb]], n, fb, 16)
                    rstd_from(0, n, rt, RT, rstd, RSTD, D)
                    resid(GG1, w, n)
                    def fin_store():
                        if l == L - 1:
                            S.dma(S.sp, outT[:, :, t0 - TC:t0 - TC + n], Xb[:, :, :n], XBB, reads=[XBB])
                        else:
                            S.dma(S.sp, XS[:, :, t0:t0 + n], Xb[:, :, :n], XBB, reads=[XBB])
                    if CUT == 8:
                        for _ in range(128):
                            wget()
                        fin_store(); continue
                    for c in range(16):
                        stat_add(Xb[:, c, :n], [XBB], n, c, 16)
                    rstd_from(0, n, rt, RT, rstd, RSTD, D)
                    for c in range(16):
                        k = c % 2
                        S.op(S.dve, lambda: nc.vector.scalar_tensor_tensor(out=tmp[k][:, :n], in0=Xb[:, c, :n], scalar=GSC2[l][:, c, w:w + 1], in1=rstd[:, :n], op0=ALU.mult, op1=ALU.mult),
                             reads=[XBB, RSTD, DERB[l]], writes=[TMP[k]])
                        S.op(S.act, lambda: nc.scalar.activation(out=A16[:, c, :n], in_=tmp[k][:, :n], func=AF.Identity, bias=MOD[l][:, 48 + c, w:w + 1], scale=1.0),
                             reads=[TMP[k], MODB[l]], writes=[A16B], partial=True)
                    if CUT == 9:
                        for _ in range(128):
                            wget()
                        fin_store(); continue
                    for hb in range(64):
                        s = wget()
                        pb = nextps()
                        pv = PS[pb][:, :n]
                        S.op(S.pe, mm16(pv, s, lambda kc: A16[:, kc, :n]), reads=[RB[s], A16B], writes=[PSB[pb]])
                        k = hb % 2
                        S.op(S.act, lambda: nc.scalar.activation(out=RL[k][:, :n], in_=pv, func=AF.Relu), reads=[PSB[pb]], writes=[RLB[k]])
                        S.op(S.dve, lambda: nc.vector.tensor_tensor(out=HID[:, hb, :n], in0=RL[k][:, :n], in1=RL[k][:, :n], op=ALU.mult), reads=[RLB[k]], writes=[HIDB[hb]])
                    if CUT == 10:
                        for _ in range(64):
                            wget()
                        fin_store(); continue
                    for fb in range(16):
                        pb = nextps()
                        pv = PS[pb][:, :n]
                        ss = wget(4)

                        def mm64(ss=ss, pv=pv):
                            for kg, s_ in enumerate(ss):
                                for kc in range(16):
                                    ins = nc.tensor.matmul(pv, lhsT=ring[s_][:, kc, :], rhs=HID[:, kg * 16 + kc, :n],
                                                           start=(kg == 0 and kc == 0), stop=(kg == 3 and kc == 15))
                            return ins
                        S.op(S.pe, mm64, reads=[RB[s_] for s_ in ss] + HIDB, writes=[PSB[pb]])
                        S.op(S.act, lambda: nc.scalar.copy(out=S1[:, fb, :n], in_=pv), reads=[PSB[pb]], writes=[S1B[fb]])
                        stat_add(S1[:, fb, :n], [S1B[fb]], n, fb, 16)
                    rstd_from(0, n, rt, RT, rstd, RSTD, D)
                    resid(GG2, w, n)
                    if l == L - 1:
                        S.dma(S.sp, outT[:, :, t0 - TC:t0 - TC + n], Xb[:, :, :n], XBB, reads=[XBB])
                    else:
                        S.dma(S.sp, XS[:, :, t0:t0 + n], Xb[:, :, :n], XBB, reads=[XBB])
                S.barrier()
                S.release([XBB, A16B])

        phase_adaln()
        for l in range(L):
            if stop_after == ("adaln", l):
                break
            phase_inproj(l)
            if stop_after == ("inproj", l):
                break
            phase_conv(l)
            if stop_after == ("conv", l):
                break
            phase_mlstm(l)
            if stop_after == ("mlstm", l):
                break
            phase_outffn(l)
            if stop_after == ("outffn", l):
                break
        S.barrier()
    return nc


def _tiles(w, tile_cols):
    K, N = w.shape
    return np.ascontiguousarray(w.reshape(K // 128, 128, N // tile_cols, tile_cols).transpose(2, 1, 0, 3))


def _col(v):
    return np.ascontiguousarray(v.reshape(-1, 128).T)


def prep_shared(inp):
    f = np.float32
    sh = {}
    sh["wada"] = np.stack([_tiles(inp["w_ada"][l], 128) for l in range(L)])
    sp = [0, 512, 1024, 2048, 3072, 4096, 5120, 5152, 5664, 6176, 6688]
    winf, wint, wing, wout, wff1, wff2, vecs = [], [], [], [], [], [], []
    for l in range(L):
        w = inp["w_in"][l]
        seg = lambda i: w[:, sp[i]:sp[i + 1]]
        a_val, a_gate, q, k, v, o, g, s_in, s_b, s_c = [seg(i) for i in range(10)]
        cols = []
        for i in range(4):
            cols += [a_gate[:, i * 128:(i + 1) * 128], a_val[:, i * 128:(i + 1) * 128]]
        cols += [q, k, o]
        for i in range(4):
            cols += [s_in[:, i * 128:(i + 1) * 128], s_c[:, i * 128:(i + 1) * 128]]
        cols += [s_b]
        winf.append(_tiles(np.concatenate(cols, axis=1), 128))
        wint.append(_tiles(np.concatenate([v, k], axis=1), 512))
        wing.append(_tiles(g, 32)[0])
        wout.append(_tiles(inp["w_out"][l], 128))
        wff1.append(_tiles(inp["w_ff1"][l], 128))
        w2 = inp["w_ff2"][l]
        t2 = w2.reshape(4, 16, 128, 16, 128).transpose(3, 0, 2, 1, 4).reshape(64, 128, 16, 128)
        wff2.append(np.ascontiguousarray(t2))
        vv = np.zeros((128, NV), f)
        vv[:, V_GPM:V_GPM + 16] = _col(inp["g_pre_mix"][l])
        vv[:, V_GQM:V_GQM + 16] = _col(inp["g_post_mix"][l])
        vv[:, V_GPF:V_GPF + 16] = _col(inp["g_pre_ffn"][l])
        vv[:, V_GQF:V_GQF + 16] = _col(inp["g_post_ffn"][l])
        vv[:, V_BADA:V_BADA + 96] = _col(inp["b_ada"][l])
        caw = inp["conv_a_w"][l]
        vv[:, V_CAW:V_CAW + 124] = caw.reshape(31, 4, 128).transpose(2, 1, 0).reshape(128, 124)
        vv[:, V_CAB:V_CAB + 4] = _col(inp["conv_a_b"][l])
        vv[:, V_LNW:V_LNW + 4] = _col(inp["ln_a_w"][l])
        vv[:, V_LNB:V_LNB + 4] = _col(inp["ln_a_b"][l])
        vv[:, V_MNW:V_MNW + 8] = _col(inp["mlstm_norm_w"][l])
        ccw = inp["conv_c_w"][l]
        vv[:, V_CCW:V_CCW + 12] = ccw.reshape(3, 4, 128).transpose(2, 1, 0).reshape(128, 12)
        vv[:, V_BG:V_BG + 32] = np.broadcast_to(inp["b_gates"][l][None, :], (128, 32))
        vecs.append(vv)
    sh["winf"] = np.stack(winf); sh["wint"] = np.stack(wint); sh["wing"] = np.stack(wing)
    sh["wout"] = np.stack(wout); sh["wff1"] = np.stack(wff1); sh["wff2"] = np.stack(wff2)
    sh["vecs"] = np.stack(vecs)
    c = np.zeros((128, 768), f)
    c[:, C_ID:C_ID + 128] = np.eye(128, dtype=f)
    c[:, C_ONE:C_ONE + 128] = 1.0
    ii = np.arange(128)
    mf = (ii[:, None] <= ii[None, :]).astype(f)
    mb = (ii[:, None] >= ii[None, :]).astype(f)
    c[:, C_MF:C_MF + 128] = mf
    c[:, C_MB:C_MB + 128] = mb
    c[:, C_NF:C_NF + 128] = (mf - 1.0) * 30000.0
    c[:, C_NB:C_NB + 128] = (mb - 1.0) * 30000.0
    sh["consts"] = c
    return sh


def prep_core(inp, b):
    X = np.concatenate([inp["ctx"][b], inp["x"][b]], axis=0)
    xT = np.ascontiguousarray(X.T.reshape(16, 128, T).transpose(1, 0, 2))
    cvv = np.stack([_col(inp["c_ctx"]), _col(inp["c"][b])], axis=-1)
    return {"xT": xT, "cv": np.ascontiguousarray(cvv.astype(np.float32))}


_NC_CACHE = {}


def kernel(**inputs):
    inp = {k: np.asarray(v, dtype=np.float32) for k, v in inputs.items()}
    sh = prep_shared(inp)
    in_maps = []
    for b in range(8):
        m = dict(sh)
        m.update(prep_core(inp, b))
        in_maps.append(m)
    if "nc" not in _NC_CACHE:
        _NC_CACHE["nc"] = build()
    res = run_bass_kernel_spmd(_NC_CACHE["nc"], in_maps, core_ids=list(range(8)))
    out = np.empty((8, 2048, 2048), np.float32)
    for b in range(8):
        oT = res.results[b]["outT"]
        out[b] = oT.transpose(2, 1, 0).reshape(2048, 2048)
    return out
```

```python
import contextlib
import numpy as np
import concourse.bass as bass
import concourse.mybir as mybir
from concourse.bass_utils import run_bass_kernel_spmd

F32 = mybir.dt.float32
BF16 = mybir.dt.bfloat16
AF = mybir.ActivationFunctionType
ALU = mybir.AluOpType

L = 2
D = 2048
T = 2304
TC = 256
EPS = 1e-6
NV = 348
CUT = 0
USE_WB = False
TBS = [(0, 256), (256, 512), (768, 512), (1280, 512), (1792, 512)]
V_GPM, V_GQM, V_GPF, V_GQF, V_BADA, V_CAW, V_CAB, V_LNW, V_LNB, V_MNW, V_CCW, V_BG = 0, 16, 32, 48, 64, 160, 284, 288, 292, 296, 304, 316
C_ONE, C_MF, C_MB, C_NF, C_NB, C_ID = 0, 128, 256, 384, 512, 640


class Buf:
    def __init__(self, name, dsem=None):
        self.name = name
        self.w = {}
        self.r = {}
        self.dsem = dsem


class EngS:
    def __init__(self, name, h, sem, same_sync):
        self.name = name
        self.h = h
        self.sem = sem
        self.count = 0
        self.waited = {}
        self.same_sync = same_sync


class Sync:
    def __init__(self, nc, es, n_dma_sems=80):
        self.nc = nc
        mk = lambda n: es.enter_context(nc.semaphore(n))
        self.pe = EngS("pe", nc.tensor, mk("s_pe"), False)
        self.act = EngS("act", nc.scalar, mk("s_act"), True)
        self.dve = EngS("dve", nc.vector, mk("s_dve"), True)
        self.pool = EngS("pool", nc.gpsimd, mk("s_pool"), True)
        self.sp = EngS("sp", nc.sync, mk("s_sp"), False)
        self.engs = [self.pe, self.act, self.dve, self.pool, self.sp]
        self.dsems = [[mk(f"s_d{i}"), 0] for i in range(n_dma_sems)]
        self.free_dsems = list(range(n_dma_sems))
        self.nbuf = 0

    def buf(self, name=None, dma=False):
        self.nbuf += 1
        b = Buf(name or f"b{self.nbuf}")
        if dma:
            b.dsem = self.free_dsems.pop(0)
        return b

    def release(self, bufs):
        for b in bufs:
            if b.dsem is not None:
                self.free_dsems.append(b.dsem)
                b.dsem = None

    def _wait(self, e, sem, val):
        if e.waited.get(sem.num, 0) >= val:
            return
        e.h.wait_ge(sem, val)
        e.waited[sem.num] = val

    def _deps(self, e, reads, writes):
        deps = {}

        def add(d, war=False):
            for k, (sem, val) in d.items():
                if k == e.sem.num and (war or not e.same_sync):
                    continue
                if deps.get(k, (None, 0))[1] < val:
                    deps[k] = (sem, val)

        for b in reads:
            add(b.w)
        for b in writes:
            add(b.w)
            add(b.r, war=True)
        for k, (sem, val) in deps.items():
            self._wait(e, sem, val)

    def _record(self, ev, reads, writes, partial):
        sem, val = ev
        for b in reads:
            b.r[sem.num] = ev
        for b in writes:
            if partial:
                b.w[sem.num] = ev
            else:
                b.w = {sem.num: ev}
                b.r = {}

    def op(self, e, fn, reads=(), writes=(), partial=False):
        self._deps(e, reads, writes)
        ins = fn()
        e.count += 1
        ins.then_inc(e.sem, 1)
        self._record((e.sem, e.count), reads, writes, partial)

    def dma(self, q, out, in_, sb, reads=(), writes=(), partial=False):
        self._deps(q, reads, writes)
        ds = self.dsems[sb.dsem]
        q.h.dma_start(out=out, in_=in_).then_inc(ds[0], 16)
        ds[1] += 16
        self._record((ds[0], ds[1]), reads, writes, partial)

    def barrier(self):
        evs = [(e.sem, e.count) for e in self.engs if e.count > 0]
        evs += [(s, v) for s, v in self.dsems if v > 0]
        for e in self.engs:
            for sem, val in evs:
                if sem.num == e.sem.num:
                    continue
                self._wait(e, sem, val)


def build(debug=False, stop_after=None):
    nc = bass.Bass("TRN2", target_bir_lowering=False)
    din = lambda name, shape, dt=F32: nc.dram_tensor(name, shape, dt, kind="ExternalInput").ap()
    skind = "ExternalOutput" if debug else "Internal"
    dsc = lambda name, shape, dt=F32: nc.dram_tensor(name, shape, dt, kind=skind).ap()
    xT = din("xT", [128, 16, T])
    cv = din("cv", [128, 16, 2])
    consts = din("consts", [128, 768])
    vecs = din("vecs", [L, 128, NV])
    wada = din("wada", [L, 96, 128, 16, 128])
    winf = din("winf", [L, 44, 128, 16, 128])
    wint = din("wint", [L, 4, 128, 16, 512])
    wing = din("wing", [L, 128, 16, 32])
    wout = din("wout", [L, 16, 128, 16, 128])
    wff1 = din("wff1", [L, 64, 128, 16, 128])
    wff2 = din("wff2", [L, 64, 128, 16, 128])
    outT = nc.dram_tensor("outT", [128, 16, 2048], F32, kind="ExternalOutput").ap()
    XS = dsc("XS", [128, 16, T])
    UA = dsc("UA", [128, 4, T])
    UC = dsc("UC", [128, 4, T])
    SBS = dsc("SBS", [128, 4, T])
    SO = dsc("SO", [128, 8, T])
    QT = dsc("QT", [128, 8, T], BF16)
    KT = dsc("KT", [128, 8, T], BF16)
    KM = dsc("KM", [T, 1024], BF16)
    VM = dsc("VM", [T, 1024], BF16)
    GT = dsc("GT", [T, 32])
    MIXT = dsc("MIXT", [128, 16, T], BF16)
    HS = dsc("HS", [128, 8, T])
    WB = nc.dram_tensor("WB", [144, 128, 16, 128], BF16, kind="Internal").ap()
    if debug:
        DMOD = dsc("DMOD", [L, 128, 96, 2])
        DHT = dsc("DHT", [128, 16, T], BF16)

    with contextlib.ExitStack() as es:
        S = Sync(nc, es)

        uid = {"n": 0}

        def sbt(name, shape, dt=F32, st=None):
            uid["n"] += 1
            return (st or es).enter_context(nc.sbuf_tensor(f"{name}_u{uid['n']}", shape, dt))

        P4 = [es.enter_context(nc.psum_tensor(f"pp{i}", [128, 1024], F32)) for i in range(4)]
        PS = [P4[i // 2][:, (i % 2) * 512:(i % 2 + 1) * 512] for i in range(8)]
        PSB = [S.buf(f"ps{i}") for i in range(8)]

        CONST = sbt("const", [128, 768])
        CONSTB = S.buf("const", dma=True)
        S.dma(S.sp, CONST[:], consts, CONSTB, writes=[CONSTB])
        ones32 = CONST[:, C_ONE:C_ONE + 128]
        onesb = sbt("onesb", [128, 128], BF16)
        ONESB = S.buf("onesb")
        S.op(S.dve, lambda: nc.vector.tensor_copy(out=onesb[:], in_=ones32), reads=[CONSTB], writes=[ONESB])
        VEC = [sbt(f"vec{l}", [128, NV]) for l in range(L)]
        VECB = S.buf("vec", dma=True)
        for l in range(L):
            S.dma(S.sp, VEC[l][:], vecs[l], VECB, writes=[VECB], partial=True)
        MOD = [sbt(f"mod{l}", [128, 96, 2]) for l in range(L)]
        MODB = [S.buf(f"mod{l}", dma=True) for l in range(L)]
        GSC1 = [sbt(f"gsc1{l}", [128, 16, 2]) for l in range(L)]
        GG1 = [sbt(f"gg1{l}", [128, 16, 2]) for l in range(L)]
        GSC2 = [sbt(f"gsc2{l}", [128, 16, 2]) for l in range(L)]
        GG2 = [sbt(f"gg2{l}", [128, 16, 2]) for l in range(L)]
        DERB = [S.buf(f"der{l}") for l in range(L)]

        NR = 8
        ring = [sbt(f"ring{i}", [128, 16, 128], BF16) for i in range(NR)]
        RB = [S.buf(f"ring{i}", dma=True) for i in range(NR)]
        WQ = []
        WBD = S.buf("WB_dram")
        WQ += [(wada[0, t], "cast", None) for t in range(32)]
        ada_rest = [(wada[0, t], "cast", None) for t in range(32, 96)] + [(wada[1, t], "cast", None) for t in range(96)]
        APUMP = 5
        for l in range(L):
            if l == 0:
                WQ += [(winf[0, t], "cast", None) for t in range(8)]
                for t in range(8, 44):
                    WQ.append((winf[0, t], "cast", None))
                    WQ += ada_rest[:APUMP]
                    ada_rest = ada_rest[APUMP:]
                assert not ada_rest
            else:
                WQ += [(winf[l, t], "cast", None) for t in range(44)]
            nblk = 5 if l == 0 else 4
            for b_ in range(nblk):
                srcs = [wout[l, t] for t in range(16)] + [wff1[l, t] for t in range(64)] + [wff2[l, t] for t in range(64)]
                for j, sap in enumerate(srcs):
                    if USE_WB:
                        WQ.append((sap, "cast_store", j) if b_ == 0 else (None, "bf16", j))
                    else:
                        WQ.append((sap, "cast", None))
        wstate = {"issued": 0, "cur": 0}

        def wget(k=None):
            cur = wstate["cur"]
            while wstate["issued"] < min(len(WQ), cur + NR):
                i = wstate["issued"]
                s = i % NR
                src, mode, j = WQ[i]
                if mode == "bf16":
                    S.dma(S.pool, ring[s][:], WB[j], RB[s], reads=[WBD], writes=[RB[s]])
                else:
                    S.dma(S.pool, ring[s][:], src, RB[s], writes=[RB[s]])
                    if mode == "cast_store":
                        S.dma(S.sp, WB[j], ring[s][:], RB[s], reads=[RB[s]], writes=[WBD], partial=True)
                wstate["issued"] += 1
            if k is None:
                wstate["cur"] += 1
                return cur % NR
            wstate["cur"] += k
            return [(cur + j) % NR for j in range(k)]

        psrot = {"i": 0}

        def nextps(lo=2, hi=8):
            i = lo + psrot["i"] % (hi - lo)
            psrot["i"] += 1
            return i

        def mm16(pv, s, rhs_of_kc, start=True, stop=True):
            def f():
                for kc in range(16):
                    ins = nc.tensor.matmul(pv, lhsT=ring[s][:, kc, :], rhs=rhs_of_kc(kc),
                                           start=(start and kc == 0), stop=(stop and kc == 15))
                return ins
            return f

        cvt = sbt("cvt", [128, 16, 2], F32)
        cvb = sbt("cvb", [128, 16, 2], BF16)
        CV = S.buf("cv", dma=True)
        CVB = S.buf("cvb")

        def mod_finish(l, psm_i, t_lo, t_hi):
            psm = PS[psm_i]
            S.op(S.dve, lambda: nc.vector.tensor_tensor(
                out=MOD[l][:, t_lo:t_hi, :], in0=psm[:, 2 * t_lo:2 * t_hi].rearrange("p (t w) -> p t w", w=2),
                in1=VEC[l][:, V_BADA + t_lo:V_BADA + t_hi].unsqueeze(2).broadcast_to([128, t_hi - t_lo, 2]), op=ALU.add),
                reads=[PSB[psm_i], VECB], writes=[MODB[l]], partial=True)

        def derived(l, which):
            bc = lambda off: VEC[l][:, off:off + 16].unsqueeze(2).broadcast_to([128, 16, 2])
            if which == 0:
                S.op(S.dve, lambda: nc.vector.scalar_tensor_tensor(out=GSC1[l][:], in0=MOD[l][:, 16:32, :], scalar=1.0, in1=bc(V_GPM), op0=ALU.add, op1=ALU.mult),
                     reads=[MODB[l], VECB], writes=[DERB[l]], partial=True)
            else:
                S.op(S.dve, lambda: nc.vector.tensor_tensor(out=GG1[l][:], in0=MOD[l][:, 32:48, :], in1=bc(V_GQM), op=ALU.mult),
                     reads=[MODB[l], VECB], writes=[DERB[l]], partial=True)
                S.op(S.dve, lambda: nc.vector.scalar_tensor_tensor(out=GSC2[l][:], in0=MOD[l][:, 64:80, :], scalar=1.0, in1=bc(V_GPF), op0=ALU.add, op1=ALU.mult),
                     reads=[MODB[l], VECB], writes=[DERB[l]], partial=True)
                S.op(S.dve, lambda: nc.vector.tensor_tensor(out=GG2[l][:], in0=MOD[l][:, 80:96, :], in1=bc(V_GQF), op=ALU.mult),
                     reads=[MODB[l], VECB], writes=[DERB[l]], partial=True)

        def adaln_tile(psm_i, t, first):
            s = wget()
            S.op(S.pe, mm16(PS[psm_i][:, 2 * t:2 * t + 2], s, lambda kc: cvb[:, kc, :]),
                 reads=[RB[s], CVB], writes=[PSB[psm_i]], partial=(not first))

        def phase_adaln():
            S.dma(S.sp, cvt[:], cv, CV, writes=[CV])
            S.op(S.act, lambda: nc.scalar.activation(out=cvb[:], in_=cvt[:], func=AF.Silu), reads=[CV], writes=[CVB])
            for t in range(32):
                adaln_tile(0, t, t == 0)
            mod_finish(0, 0, 0, 32)
            derived(0, 0)
            S.barrier()

        def adaln_rest():
            for t in range(32, 96):
                adaln_tile(6, t, t == 32)
                yield
            for t in range(96):
                adaln_tile(7, t, t == 0)
                yield
            mod_finish(0, 6, 32, 96)
            derived(0, 1)
            mod_finish(1, 7, 0, 96)
            derived(1, 0)
            derived(1, 1)
            if debug:
                for l in range(L):
                    S.dma(S.sp, DMOD[l], MOD[l][:], MODB[l], reads=[MODB[l]])

        def rstd_from(ps_i, n, rt, RT, rstd, RSTD, div):
            S.op(S.act, lambda: nc.scalar.activation(out=rt[:, :n], in_=PS[ps_i][:, :n], func=AF.Sqrt, bias=EPS, scale=1.0 / div),
                 reads=[PSB[ps_i]], writes=[RT])
            S.op(S.dve, lambda: nc.vector.reciprocal(out=rstd[:, :n], in_=rt[:, :n]), reads=[RT], writes=[RSTD])

        def phase_inproj(l):
            xsrc = xT if l == 0 else XS
            with contextlib.ExitStack() as st:
                hT = sbt("hT", [128, 16, T], BF16, st)
                HT = S.buf("hT", dma=True)
                xb = sbt("xb", [128, 16, 512], F32, st)
                XB = S.buf("xb", dma=True)
                sq = [sbt(f"sq{i}", [128, 512], F32, st) for i in range(2)]
                SQ = [S.buf(f"sq{i}") for i in range(2)]
                tmp = [sbt(f"tmp{i}", [128, 512], F32, st) for i in range(2)]
                TMP = [S.buf(f"tmp{i}") for i in range(2)]
                rt = sbt("rt", [128, 512], F32, st); RT = S.buf("rt")
                rstd = sbt("rstd", [128, 512], F32, st); RSTD = S.buf("rstd")
                def norm_block(bi, t0, n):
                        w = 0 if bi == 0 else 1
                        S.dma(S.sp, xb[:, :, :n], xsrc[:, :, t0:t0 + n], XB, writes=[XB])
                        for c in range(16):
                            k = c % 2
                            S.op(S.act, lambda: nc.scalar.activation(out=sq[k][:, :n], in_=xb[:, c, :n], func=AF.Square), reads=[XB], writes=[SQ[k]])
                            S.op(S.pe, lambda: nc.tensor.matmul(PS[0][:, :n], lhsT=ones32, rhs=sq[k][:, :n], start=(c == 0), stop=(c == 15)),
                                 reads=[SQ[k], CONSTB], writes=[PSB[0]], partial=(c > 0))
                        rstd_from(0, n, rt, RT, rstd, RSTD, D)
                        for c in range(16):
                            k = c % 2
                            S.op(S.dve, lambda: nc.vector.scalar_tensor_tensor(out=tmp[k][:, :n], in0=xb[:, c, :n], scalar=GSC1[l][:, c, w:w + 1], in1=rstd[:, :n], op0=ALU.mult, op1=ALU.mult),
                                 reads=[XB, RSTD, DERB[l]], writes=[TMP[k]])
                            S.op(S.act, lambda: nc.scalar.activation(out=hT[:, c, t0:t0 + n], in_=tmp[k][:, :n], func=AF.Identity, bias=MOD[l][:, c, w:w + 1], scale=1.0),
                                 reads=[TMP[k], MODB[l]], writes=[HT], partial=True)

                hold = sbt("hold", [128, T], F32, st); HOLD = S.buf("hold")
                s32 = [sbt(f"s32_{i}", [128, 512], F32, st) for i in range(2)]
                S32 = [S.buf(f"s32_{i}", dma=True) for i in range(2)]
                s16 = [sbt(f"s16_{i}", [128, 512], BF16, st) for i in range(3)]
                S16 = [S.buf(f"s16_{i}", dma=True) for i in range(3)]
                rot = {"a": 0, "b": 0}
                kinds = []
                for i in range(4):
                    kinds += [("ag", i), ("av", i)]
                kinds += [("q", i) for i in range(8)] + [("k", i) for i in range(8)] + [("o", i) for i in range(8)]
                for i in range(4):
                    kinds += [("sin", i), ("sc", i)]
                kinds += [("sb", i) for i in range(4)]
                def do_tile(kind, idx, s, bi, t0, n):
                    pb = nextps(2, 6 if l == 0 else 8)
                    pv = PS[pb][:, :n]
                    S.op(S.pe, mm16(pv, s, lambda kc: hT[:, kc, t0:t0 + n]), reads=[RB[s], HT], writes=[PSB[pb]])
                    if kind in ("ag", "sin"):
                        fn = AF.Sigmoid if kind == "ag" else AF.Copy
                        S.op(S.act, lambda: nc.scalar.activation(out=hold[:, t0:t0 + n], in_=pv, func=fn), reads=[PSB[pb]], writes=[HOLD], partial=True)
                    elif kind in ("av", "sc"):
                        k = rot["a"] % 2; rot["a"] += 1
                        S.op(S.dve, lambda: nc.vector.tensor_tensor(out=s32[k][:, :n], in0=pv, in1=hold[:, t0:t0 + n], op=ALU.mult),
                             reads=[PSB[pb], HOLD], writes=[S32[k]])
                        dst = UA if kind == "av" else UC
                        S.dma(S.sp, dst[:, idx, t0:t0 + n], s32[k][:, :n], S32[k], reads=[S32[k]])
                    elif kind in ("q", "k"):
                        k = rot["b"] % 3; rot["b"] += 1
                        sc_ = 1.0 if kind == "q" else 128.0 ** -0.5
                        S.op(S.act, lambda: nc.scalar.mul(out=s16[k][:, :n], in_=pv, mul=sc_), reads=[PSB[pb]], writes=[S16[k]])
                        dst = QT if kind == "q" else KT
                        S.dma(S.sp, dst[:, idx, t0:t0 + n], s16[k][:, :n], S16[k], reads=[S16[k]])
                    else:
                        k = rot["a"] % 2; rot["a"] += 1
                        fn = AF.Sigmoid if kind == "o" else AF.Copy
                        S.op(S.act, lambda: nc.scalar.activation(out=s32[k][:, :n], in_=pv, func=fn), reads=[PSB[pb]], writes=[S32[k]])
                        dst = SO if kind == "o" else SBS
                        S.dma(S.sp, dst[:, idx, t0:t0 + n], s32[k][:, :n], S32[k], reads=[S32[k]])

                first8 = wget(8)
                for bi, (t0, n) in enumerate(TBS):
                    norm_block(bi, t0, n)
                    if l == 1 and bi == 0:
                        continue
                    for j_ in range(8):
                        do_tile(kinds[j_][0], kinds[j_][1], first8[j_], bi, t0, n)
                if debug and l == 0:
                    S.dma(S.sp, DHT, hT[:], HT, reads=[HT])
                agen = adaln_rest() if l == 0 else None
                for kind, idx in kinds[8:]:
                    s = wget()
                    for bi, (t0, n) in enumerate(TBS):
                        if l == 1 and bi == 0:
                            continue
                        do_tile(kind, idx, s, bi, t0, n)
                    if agen is not None:
                        for _ in range(APUMP):
                            next(agen, None)
                if agen is not None:
                    for _ in range(20):
                        next(agen, None)
                wide = [sbt(f"wide{i}", [128, 16, 512], BF16, st) for i in range(2)]
                WIDE = [S.buf(f"wide{i}", dma=True) for i in range(2)]
                gtile = sbt("gtile", [128, 16, 32], BF16, st); GTILE = S.buf("gtile", dma=True)
                sg = [sbt(f"sg{i}", [128, 32], F32, st) for i in range(2)]
                SG = [S.buf(f"sg{i}", dma=True) for i in range(2)]
                S.dma(S.pool, gtile[:], wing[l], GTILE, writes=[GTILE])
                for wi in range(4):
                    wb = wi % 2
                    for g4 in range(4):
                        S.dma(S.pool, wide[wb][:, 4 * g4:4 * g4 + 4, :], wint[l, wi, :, 4 * g4:4 * g4 + 4, :], WIDE[wb], writes=[WIDE[wb]], partial=(g4 > 0))
                    for tt in range(18):
                        pb = nextps()

                        def mmw(pb=pb, tt=tt, wb=wb):
                            for kc in range(16):
                                ins = nc.tensor.matmul(PS[pb][:, :], lhsT=hT[:, kc, tt * 128:(tt + 1) * 128], rhs=wide[wb][:, kc, :],
                                                       start=(kc == 0), stop=(kc == 15))
                            return ins
                        S.op(S.pe, mmw, reads=[WIDE[wb], HT], writes=[PSB[pb]])
                        k = rot["b"] % 3; rot["b"] += 1
                        sc_ = 1.0 if wi < 2 else 128.0 ** -0.5
                        S.op(S.act, lambda: nc.scalar.mul(out=s16[k][:, :], in_=PS[pb][:, :], mul=sc_), reads=[PSB[pb]], writes=[S16[k]])
                        dst = VM if wi < 2 else KM
                        S.dma(S.sp, dst[tt * 128:(tt + 1) * 128, wb * 512:(wb + 1) * 512], s16[k][:, :], S16[k], reads=[S16[k]])
                for tt in range(18):
                    pb = nextps()

                    def mmg(pb=pb, tt=tt):
                        for kc in range(16):
                            ins = nc.tensor.matmul(PS[pb][:, 0:32], lhsT=hT[:, kc, tt * 128:(tt + 1) * 128], rhs=gtile[:, kc, :],
                                                   start=(kc == 0), stop=(kc == 15))
                        return ins
                    S.op(S.pe, mmg, reads=[GTILE, HT], writes=[PSB[pb]])
                    k = tt % 2
                    S.op(S.dve, lambda: nc.vector.tensor_tensor(out=sg[k][:], in0=PS[pb][:, 0:32], in1=VEC[l][:, V_BG:V_BG + 32], op=ALU.add),
                         reads=[PSB[pb], VECB], writes=[SG[k]])
                    S.dma(S.sp, GT[tt * 128:(tt + 1) * 128, :], sg[k][:], SG[k], reads=[SG[k]])
                S.barrier()
                S.release([HT, XB, GTILE] + S32 + S16 + WIDE + SG)

        def phase_conv(l):
            blocks = TBS if l == 0 else TBS[1:]
            lo = 0 if l == 0 else TC
            with contextlib.ExitStack() as st:
                Y = sbt("Y", [128, 4, T], F32, st); YB = [S.buf(f"Y{i}") for i in range(4)]
                U = [sbt(f"U{i}", [128, T], F32, st) for i in range(2)]
                UB = [S.buf(f"U{i}", dma=True) for i in range(2)]
                Y2 = sbt("Y2", [128, T], F32, st); Y2B = S.buf("Y2")
                UPL = sbt("UPL", [128, 4, 32, 94], BF16, st); UPLB = [S.buf(f"UPL{i}") for i in range(4)]
                UPC = sbt("UPC", [128, 4, 286], BF16, st); UPCB = [S.buf(f"UPC{i}") for i in range(4)]
                DG = sbt("DG", [128, 4, 31, 128], BF16, st); DGB = [S.buf(f"DG{i}") for i in range(4)]
                ident = CONST[:, C_ID:C_ID + 128]
                agen = None
                pshi = 8

                def apump(k):
                    if agen is not None:
                        for _ in range(k):
                            next(agen, None)
                S.op(S.pool, lambda: nc.gpsimd.memset(UPL[:].rearrange("p a r j -> p (a r j)"), 0.0), writes=UPLB)
                if l == 0:
                    S.op(S.pool, lambda: nc.gpsimd.memset(UPC[:].rearrange("p a j -> p (a j)"), 0.0), writes=UPCB)
                for i in range(4):
                    u = U[i % 2]
                    UBi = UB[i % 2]
                    S.dma(S.sp, u[:, lo:], UA[:, i, lo:], UBi, writes=[UBi])

                    def dgop(i=i):
                        for k in range(31):
                            ins = nc.vector.tensor_scalar_mul(out=DG[:, i, k, :], in0=ident, scalar1=VEC[l][:, V_CAW + i * 31 + k:V_CAW + i * 31 + k + 1])
                        return ins
                    S.op(S.dve, dgop, reads=[CONSTB, VECB], writes=[DGB[i]])
                    S.op(S.act, lambda: nc.scalar.copy(out=UPL[:, i, :, 15:79], in_=u[:, TC:].rearrange("p (r j) -> p r j", j=64)),
                         reads=[UBi], writes=[UPLB[i]], partial=True)
                    if l == 0:
                        S.op(S.act, lambda: nc.scalar.copy(out=UPC[:, i, 15:271], in_=u[:, 0:TC]), reads=[UBi], writes=[UPCB[i]], partial=True)
                    bias = VEC[l][:, V_CAB + i:V_CAB + i + 1]
                    for rb in range(4):
                        pb = nextps(2, pshi)
                        pv = PS[pb].rearrange("p (r j) -> p r j", j=64)

                        def mmc(i=i, rb=rb, pv=pv):
                            for k in range(31):
                                ins = nc.tensor.matmul(pv, lhsT=DG[:, i, k, :], rhs=UPL[:, i, rb * 8:rb * 8 + 8, k:k + 64], start=(k == 0), stop=(k == 30))
                            return ins
                        S.op(S.pe, mmc, reads=[DGB[i], UPLB[i]], writes=[PSB[pb]])
                        t0_ = TC + rb * 512
                        S.op(S.act, lambda: nc.scalar.activation(out=Y[:, i, t0_:t0_ + 512], in_=PS[pb], func=AF.Identity, bias=bias, scale=1.0),
                             reads=[PSB[pb], VECB], writes=[YB[i]], partial=True)
                        apump(8)
                    if l == 0:
                        pb = nextps(2, pshi)

                        def mmcc(i=i, pb=pb):
                            for k in range(31):
                                ins = nc.tensor.matmul(PS[pb][:, 0:256], lhsT=DG[:, i, k, :], rhs=UPC[:, i, k:k + 256], start=(k == 0), stop=(k == 30))
                            return ins
                        S.op(S.pe, mmcc, reads=[DGB[i], UPCB[i]], writes=[PSB[pb]])
                        S.op(S.act, lambda: nc.scalar.activation(out=Y[:, i, 0:256], in_=PS[pb][:, 0:256], func=AF.Identity, bias=bias, scale=1.0),
                             reads=[PSB[pb], VECB], writes=[YB[i]], partial=True)
                apump(200)
                sq = [sbt(f"csq{i}", [128, 512], F32, st) for i in range(2)]
                SQ = [S.buf(f"csq{i}") for i in range(2)]
                mean = sbt("cmean", [128, 512], F32, st); MEAN = S.buf("cmean")
                msq = sbt("cmsq", [128, 512], F32, st); MSQ = S.buf("cmsq")
                var = sbt("cvar", [128, 512], F32, st); VAR = S.buf("cvar")
                rt = sbt("crt", [128, 512], F32, st); RT = S.buf("crt")
                rstd = sbt("crstd", [128, 512], F32, st); RSTD = S.buf("crstd")
                t1 = [sbt(f"ct1{i}", [128, 512], F32, st) for i in range(2)]
                T1 = [S.buf(f"ct1{i}") for i in range(2)]
                o16 = [sbt(f"co16{i}", [128, 512], BF16, st) for i in range(2)]
                O16 = [S.buf(f"co16{i}", dma=True) for i in range(2)]
                for (t0, n) in blocks:
                    for i in range(4):
                        k = i % 2
                        S.op(S.act, lambda: nc.scalar.activation(out=sq[k][:, :n], in_=Y[:, i, t0:t0 + n], func=AF.Square), reads=[YB[i]], writes=[SQ[k]])
                        S.op(S.pe, lambda: nc.tensor.matmul(PS[0][:, :n], lhsT=ones32, rhs=sq[k][:, :n], start=(i == 0), stop=(i == 3)),
                             reads=[SQ[k], CONSTB], writes=[PSB[0]], partial=(i > 0))
                        S.op(S.pe, lambda: nc.tensor.matmul(PS[1][:, :n], lhsT=ones32, rhs=Y[:, i, t0:t0 + n], start=(i == 0), stop=(i == 3)),
                             reads=[YB[i], CONSTB], writes=[PSB[1]], partial=(i > 0))
                    S.op(S.dve, lambda: nc.vector.tensor_scalar_mul(out=mean[:, :n], in0=PS[1][:, :n], scalar1=1.0 / 512), reads=[PSB[1]], writes=[MEAN])
                    S.op(S.dve, lambda: nc.vector.tensor_tensor(out=msq[:, :n], in0=mean[:, :n], in1=mean[:, :n], op=ALU.mult), reads=[MEAN], writes=[MSQ])
                    S.op(S.dve, lambda: nc.vector.scalar_tensor_tensor(out=var[:, :n], in0=PS[0][:, :n], scalar=1.0 / 512, in1=msq[:, :n], op0=ALU.mult, op1=ALU.subtract),
                         reads=[PSB[0], MSQ], writes=[VAR])
                    S.op(S.act, lambda: nc.scalar.activation(out=rt[:, :n], in_=var[:, :n], func=AF.Sqrt, bias=EPS, scale=1.0), reads=[VAR], writes=[RT])
                    S.op(S.dve, lambda: nc.vector.reciprocal(out=rstd[:, :n], in_=rt[:, :n]), reads=[RT], writes=[RSTD])
                    for i in range(4):
                        k = i % 2
                        S.op(S.dve, lambda: nc.vector.tensor_tensor(out=t1[k][:, :n], in0=Y[:, i, t0:t0 + n], in1=mean[:, :n], op=ALU.subtract), reads=[YB[i], MEAN], writes=[T1[k]])
                        S.op(S.dve, lambda: nc.vector.scalar_tensor_tensor(out=t1[k][:, :n], in0=t1[k][:, :n], scalar=VEC[l][:, V_LNW + i:V_LNW + i + 1], in1=rstd[:, :n], op0=ALU.mult, op1=ALU.mult),
                             reads=[T1[k], RSTD, VECB], writes=[T1[k]])
                        S.op(S.act, lambda: nc.scalar.activation(out=o16[k][:, :n], in_=t1[k][:, :n], func=AF.Silu, bias=VEC[l][:, V_LNB + i:V_LNB + i + 1], scale=1.0),
                             reads=[T1[k], VECB], writes=[O16[k]])
                        S.dma(S.sp, MIXT[:, i, t0:t0 + n], o16[k][:, :n], O16[k], reads=[O16[k]])
                SBV = sbt("sbv", [128, T], F32, st); SBVB = S.buf("sbv", dma=True)
                for i in range(4):
                    u = U[i % 2]
                    UBi = UB[i % 2]
                    S.dma(S.sp, u[:, lo:], UC[:, i, lo:], UBi, writes=[UBi])
                    S.dma(S.sp, SBV[:, lo:], SBS[:, i, lo:], SBVB, writes=[SBVB])
                    wc = lambda k: VEC[l][:, V_CCW + i * 3 + k:V_CCW + i * 3 + k + 1]
                    S.op(S.dve, lambda: nc.vector.tensor_scalar_mul(out=Y2[:, lo:], in0=u[:, lo:], scalar1=wc(1)), reads=[UBi, VECB], writes=[Y2B])
                    rngs = [(TC, T, 64)] + ([(0, TC, 1)] if l == 0 else [])
                    for (a, b, sh) in rngs:
                        S.op(S.dve, lambda: nc.vector.scalar_tensor_tensor(out=Y2[:, a + sh:b], in0=u[:, a:b - sh], scalar=wc(0), in1=Y2[:, a + sh:b], op0=ALU.mult, op1=ALU.add),
                             reads=[UBi, VECB, Y2B], writes=[Y2B], partial=True)
                        S.op(S.dve, lambda: nc.vector.scalar_tensor_tensor(out=Y2[:, a:b - sh], in0=u[:, a + sh:b], scalar=wc(2), in1=Y2[:, a:b - sh], op0=ALU.mult, op1=ALU.add),
                             reads=[UBi, VECB, Y2B], writes=[Y2B], partial=True)
                    for (t0, n) in blocks:
                        k = (t0 // 256) % 2
                        S.op(S.dve, lambda: nc.vector.tensor_tensor(out=o16[k][:, :n], in0=Y2[:, t0:t0 + n], in1=SBV[:, t0:t0 + n], op=ALU.mult),
                             reads=[Y2B, SBVB], writes=[O16[k]])
                        S.dma(S.sp, MIXT[:, 12 + i, t0:t0 + n], o16[k][:, :n], O16[k], reads=[O16[k]])
                S.barrier()
                S.release(UB + O16 + [SBVB])

        def phase_mlstm(l):
            with contextlib.ExitStack() as st:
                HB = [S.buf(f"H{c}") for c in range(18)]
                mk = lambda n, shape, dt: sbt(n, shape, dt, st)
                LD = ["qT", "kT", "km", "vm", "g"]
                X = []
                for d in range(2):
                    x = {}
                    for nm in ["PT", "qs", "vw", "Cb", "nB"]:
                        x[nm] = mk(f"{nm}{d}", [128, 8, 128], BF16)
                    for nm in ["qT", "kT"]:
                        x[nm] = [mk(f"{nm}{d}{j}", [128, 8, 128], BF16) for j in range(2)]
                    for nm in ["km", "vm"]:
                        x[nm] = [mk(f"{nm}{d}{j}", [128, 1024], BF16) for j in range(2)]
                    x["g"] = [mk(f"g{d}{j}", [128, 32], F32) for j in range(2)]
                    for nm in ["R", "EB", "ARG", "Dm", "C32", "dn", "hh"]:
                        x[nm] = mk(f"{nm}{d}", [128, 8, 128], F32)
                    for nm in ["e1", "lp", "lf", "c1", "a2", "wtok", "n32"]:
                        x[nm] = mk(f"{nm}{d}", [128, 8], F32)
                    x["wtb"] = mk(f"wtb{d}", [128, 8], BF16)
                    x["B"] = {nm: S.buf(f"{nm}{d}", dma=(nm == "hh"))
                              for nm in ["PT", "qs", "vw", "Cb", "nB", "R", "EB", "ARG", "Dm", "C32", "dn", "hh",
                                         "e1", "lp", "lf", "c1", "a2", "wtok", "n32", "wtb"]}
                    for nm in LD:
                        x["B"][nm] = [S.buf(f"{nm}{d}{j}", dma=True) for j in range(2)]
                    x["SOc"] = mk(f"SOc{d}", [128, 8, 128], F32); x["B"]["SOc"] = S.buf(f"SOc{d}", dma=True)
                    x["hprev"] = mk(f"hprev{d}", [128, 8, 128], F32); x["B"]["hprev"] = S.buf(f"hprev{d}", dma=True)
                    for nm in ["fmean", "cen", "fsq", "frt", "frs"]:
                        x[nm] = mk(f"{nm}{d}", [128, 4, 128], F32); x["B"][nm] = S.buf(f"{nm}{d}")
                    x["m16"] = mk(f"m16{d}", [128, 8, 128], BF16); x["B"]["m16"] = S.buf(f"m16{d}", dma=True)
                    X.append(x)

                fl = lambda ap: ap.rearrange("p h e -> p (h e)")
                p3 = lambda ap: ap.rearrange("p (h e) -> p h e", e=128)

                def need(ch):
                    return l == 0 or ch >= 2

                def loads(d, ch, par):
                    x = X[d]; B = x["B"]
                    tok0 = ch * 128
                    S.dma(S.sp, x["km"][par][:], KM[tok0:tok0 + 128, :], B["km"][par], writes=[B["km"][par]])
                    S.dma(S.sp, x["vm"][par][:], VM[tok0:tok0 + 128, :], B["vm"][par], writes=[B["vm"][par]])
                    S.dma(S.sp, x["g"][par][:], GT[tok0:tok0 + 128, :], B["g"][par], writes=[B["g"][par]])
                    if need(ch):
                        S.dma(S.sp, x["qT"][par][:], QT[:, :, tok0:tok0 + 128], B["qT"][par], writes=[B["qT"][par]])
                        S.dma(S.sp, x["kT"][par][:], KT[:, :, tok0:tok0 + 128], B["kT"][par], writes=[B["kT"][par]])

                visited = set()

                def unit(d, ch, par, first, last):
                    x = X[d]; B = x["B"]
                    need_out = need(ch)
                    tok0 = ch * 128
                    bk = 4 * d
                    iBL, iST, iNM, iSM = bk, bk + 1, bk + 2, bk + 3
                    BLD, STD, NUMF, SM = PS[iBL], PS[iST], PS[iNM], PS[iSM]
                    co_m, co_n, lastc = (C_MF, C_NF, 127) if d == 0 else (C_MB, C_NB, 0)
                    Mk = CONST[:, co_m:co_m + 128]
                    NEG = CONST[:, co_n:co_n + 128]
                    gb = 0 if d == 0 else 16
                    km, vm, g, qT, kT = x["km"][par], x["vm"][par], x["g"][par], x["qT"][par], x["kT"][par]
                    Bkm, Bvm, Bg, BqT, BkT = B["km"][par], B["vm"][par], B["g"][par], B["qT"][par], B["kT"][par]
                    S.op(S.act, lambda: nc.scalar.activation(out=x["e1"][:], in_=g[:, gb + 8:gb + 16], func=AF.Exp, scale=-1.0), reads=[Bg], writes=[B["e1"]])
                    S.op(S.act, lambda: nc.scalar.activation(out=x["lp"][:], in_=x["e1"][:], func=AF.Ln, bias=1.0, scale=1.0), reads=[B["e1"]], writes=[B["lp"]])
                    yield
                    S.op(S.dve, lambda: nc.vector.tensor_scalar_mul(out=x["lf"][:], in0=x["lp"][:], scalar1=-1.0), reads=[B["lp"]], writes=[B["lf"]])
                    S.op(S.dve, lambda: nc.vector.tensor_tensor(out=x["R"][:], in0=Mk.unsqueeze(1).broadcast_to([128, 8, 128]),
                                                                in1=x["lf"][:].unsqueeze(2).broadcast_to([128, 8, 128]), op=ALU.mult),
                         reads=[CONSTB, B["lf"]], writes=[B["R"]])
                    if need_out and not first:
                        pass
                    yield
                    for half in range(2):
                        hs = slice(4 * half, 4 * half + 4)

                        def mmbl(half=half, hs=hs):
                            ins = nc.tensor.matmul(BLD, lhsT=ones32, rhs=fl(x["R"][:, hs, :]), start=True, stop=True)
                            if half == 0:
                                ins = nc.tensor.matmul(SM[:, 0:8], lhsT=Mk, rhs=x["lf"][:], start=True, stop=True)
                            return ins
                        S.op(S.pe, mmbl, reads=[B["R"], B["lf"], CONSTB], writes=[PSB[iBL]] + ([PSB[iSM]] if half == 0 else []))
                        yield
                        S.op(S.act, lambda: nc.scalar.activation(out=fl(x["EB"][:, hs, :]), in_=BLD, func=AF.Exp), reads=[PSB[iBL]], writes=[B["EB"]], partial=True)
                        if half == 0:
                            S.op(S.dve, lambda: nc.vector.scalar_tensor_tensor(out=x["c1"][:], in0=SM[:, 0:8], scalar=-1.0, in1=g[:, gb:gb + 8], op0=ALU.mult, op1=ALU.add),
                                 reads=[Bg, PSB[iSM]], writes=[B["c1"]])
                        yield

                        def argop(half=half):
                            for hh_ in range(4):
                                ins = nc.vector.tensor_tensor(out=x["ARG"][:, 4 * half + hh_, :], in0=BLD[:, hh_ * 128:(hh_ + 1) * 128], in1=NEG, op=ALU.add)
                            return ins
                        S.op(S.dve, argop, reads=[PSB[iBL], CONSTB, B["EB"]], writes=[B["ARG"]], partial=True)
                        yield
                        if need_out:
                            def dmop(half=half):
                                for hh_ in range(4):
                                    h = 4 * half + hh_
                                    ins = nc.scalar.activation(out=x["Dm"][:, h, :], in_=x["ARG"][:, h, :], func=AF.Exp, bias=x["c1"][:, h:h + 1], scale=1.0)
                                return ins
                            S.op(S.act, dmop, reads=[B["ARG"], B["c1"]], writes=[B["Dm"]], partial=True)
                            yield
                    if need_out:
                        if not first:
                            S.op(S.dve, lambda: nc.vector.tensor_tensor(out=x["qs"][:], in0=qT[:], in1=x["EB"][:], op=ALU.mult), reads=[BqT, B["EB"]], writes=[B["qs"]])
                        for half in range(2):
                            hs = slice(4 * half, 4 * half + 4)

                            def mmst(half=half):
                                for hh_ in range(4):
                                    h = 4 * half + hh_
                                    ins = nc.tensor.matmul(STD[:, hh_ * 128:(hh_ + 1) * 128], lhsT=kT[:, h, :], rhs=qT[:, h, :], start=True, stop=True)
                                return ins
                            S.op(S.pe, mmst, reads=[BkT, BqT], writes=[PSB[iST]])
                            yield
                            S.op(S.dve, lambda: nc.vector.tensor_tensor(out=x["PT"][:, hs, :], in0=p3(STD), in1=x["Dm"][:, hs, :], op=ALU.mult),
                                 reads=[PSB[iST], B["Dm"]], writes=[B["PT"]], partial=True)
                            yield

                            def mmnum(half=half):
                                for hh_ in range(4):
                                    h = 4 * half + hh_
                                    cs = slice(hh_ * 128, (hh_ + 1) * 128)
                                    ins = nc.tensor.matmul(NUMF[:, cs], lhsT=vm[:, h * 128:(h + 1) * 128], rhs=x["PT"][:, h, :], start=True, stop=first)
                                    if not first:
                                        ins = nc.tensor.matmul(NUMF[:, cs], lhsT=x["Cb"][:, h, :], rhs=x["qs"][:, h, :], start=False, stop=True)
                                for hh_ in range(4):
                                    h = 4 * half + hh_
                                    cs = slice(hh_ * 128, (hh_ + 1) * 128)
                                    ins = nc.tensor.matmul(STD[:, cs], lhsT=onesb[:], rhs=x["PT"][:, h, :], start=True, stop=first)
                                    if not first:
                                        ins = nc.tensor.matmul(STD[:, cs], lhsT=x["nB"][:, h, :], rhs=x["qs"][:, h, :], start=False, stop=True)
                                return ins
                            rd = [Bvm, B["PT"], ONESB] + ([B["Cb"], B["nB"], B["qs"]] if not first else [])
                            S.op(S.pe, mmnum, reads=rd, writes=[PSB[iNM], PSB[iST]])
                            yield
                            S.op(S.act, lambda: nc.scalar.activation(out=fl(x["dn"][:, hs, :]), in_=STD, func=AF.Abs), reads=[PSB[iST]], writes=[B["dn"]], partial=True)
                            yield
                            S.op(S.dve, lambda: nc.vector.tensor_scalar_max(out=x["dn"][:, hs, :], in0=x["dn"][:, hs, :], scalar1=1.0), reads=[B["dn"]], writes=[B["dn"]], partial=True)
                            S.op(S.dve, lambda: nc.vector.reciprocal(out=x["dn"][:, hs, :], in_=x["dn"][:, hs, :]), reads=[B["dn"]], writes=[B["dn"]], partial=True)
                            S.op(S.dve, lambda: nc.vector.tensor_tensor(out=x["hh"][:, hs, :], in0=p3(NUMF), in1=x["dn"][:, hs, :], op=ALU.mult),
                                 reads=[PSB[iNM], B["dn"]], writes=[B["hh"]], partial=True)
                            yield
                        if ch not in visited:
                            S.dma(S.sp, HS[:, :, tok0:tok0 + 128], x["hh"][:], B["hh"], reads=[B["hh"]], writes=[HB[ch]])
                        else:
                            S.dma(S.sp, x["hprev"][:], HS[:, :, tok0:tok0 + 128], B["hprev"], reads=[HB[ch]], writes=[B["hprev"]])
                            S.dma(S.sp, x["SOc"][:], SO[:, :, tok0:tok0 + 128], B["SOc"], writes=[B["SOc"]])
                    if not last:
                        S.op(S.dve, lambda: nc.vector.tensor_tensor(out=x["a2"][:], in0=x["c1"][:], in1=x["ARG"][:, :, lastc], op=ALU.add), reads=[B["c1"], B["ARG"]], writes=[B["a2"]])
                        yield
                        S.op(S.act, lambda: nc.scalar.activation(out=x["wtok"][:], in_=x["a2"][:], func=AF.Exp), reads=[B["a2"]], writes=[B["wtok"]])
                        yield
                        S.op(S.dve, lambda: nc.vector.tensor_tensor(out=x["vw"][:], in0=p3(vm[:]), in1=x["wtok"][:].unsqueeze(2).broadcast_to([128, 8, 128]), op=ALU.mult),
                             reads=[Bvm, B["wtok"]], writes=[B["vw"]])
                        S.op(S.dve, lambda: nc.vector.tensor_copy(out=x["wtb"][:], in_=x["wtok"][:]), reads=[B["wtok"]], writes=[B["wtb"]])
                        dec = x["EB"][:, :, lastc]
                        if not first:
                            S.op(S.dve, lambda: nc.vector.tensor_tensor(out=x["C32"][:], in0=x["C32"][:], in1=dec.unsqueeze(2).broadcast_to([128, 8, 128]), op=ALU.mult),
                                 reads=[B["C32"], B["EB"]], writes=[B["C32"]])
                            S.op(S.dve, lambda: nc.vector.tensor_tensor(out=x["n32"][:], in0=x["n32"][:], in1=dec, op=ALU.mult), reads=[B["n32"], B["EB"]], writes=[B["n32"]])
                        yield
                        for half in range(2):
                            hs = slice(4 * half, 4 * half + 4)

                            def mmdel(half=half):
                                for hh_ in range(4):
                                    h = 4 * half + hh_
                                    ins = nc.tensor.matmul(BLD[:, hh_ * 128:(hh_ + 1) * 128], lhsT=km[:, h * 128:(h + 1) * 128], rhs=x["vw"][:, h, :], start=True, stop=True)
                                if half == 0:
                                    for h in range(8):
                                        ins = nc.tensor.matmul(SM[:, 8 + h:9 + h], lhsT=km[:, h * 128:(h + 1) * 128], rhs=x["wtb"][:, h:h + 1], start=True, stop=True)
                                return ins
                            S.op(S.pe, mmdel, reads=[Bkm, B["vw"], B["wtb"]], writes=[PSB[iBL]] + ([PSB[iSM]] if half == 0 else []))
                            yield
                            if first:
                                S.op(S.dve, lambda: nc.vector.tensor_copy(out=x["C32"][:, hs, :], in_=p3(BLD)), reads=[PSB[iBL]], writes=[B["C32"]], partial=True)
                            else:
                                S.op(S.dve, lambda: nc.vector.tensor_tensor(out=x["C32"][:, hs, :], in0=p3(BLD), in1=x["C32"][:, hs, :], op=ALU.add),
                                     reads=[PSB[iBL], B["C32"]], writes=[B["C32"]], partial=True)
                            if half == 0:
                                if first:
                                    S.op(S.dve, lambda: nc.vector.tensor_copy(out=x["n32"][:], in_=SM[:, 8:16]), reads=[PSB[iSM]], writes=[B["n32"]])
                                else:
                                    S.op(S.dve, lambda: nc.vector.tensor_tensor(out=x["n32"][:], in0=SM[:, 8:16], in1=x["n32"][:], op=ALU.add), reads=[B["n32"], PSB[iSM]], writes=[B["n32"]])
                            yield
                        S.op(S.act, lambda: nc.scalar.copy(out=x["Cb"][:], in_=x["C32"][:]), reads=[B["C32"]], writes=[B["Cb"]])
                        S.op(S.dve, lambda: nc.vector.tensor_copy(out=x["nB"][:], in_=x["n32"][:].unsqueeze(2).broadcast_to([128, 8, 128])), reads=[B["n32"]], writes=[B["nB"]])
                        yield
                    if need_out:
                        if ch in visited:
                            S.op(S.dve, lambda: nc.vector.tensor_tensor(out=x["hh"][:], in0=x["hh"][:], in1=x["hprev"][:], op=ALU.add), reads=[B["hh"], B["hprev"]], writes=[B["hh"]])
                            yield
                            for half in range(2):
                                hs = slice(4 * half, 4 * half + 4)
                                S.op(S.pe, lambda: nc.tensor.matmul(NUMF, lhsT=ones32, rhs=fl(x["hh"][:, hs, :]), start=True, stop=True),
                                     reads=[B["hh"], CONSTB], writes=[PSB[iNM]])
                                yield
                                S.op(S.dve, lambda: nc.vector.tensor_scalar_mul(out=fl(x["fmean"][:]), in0=NUMF, scalar1=1.0 / 128), reads=[PSB[iNM]], writes=[B["fmean"]])
                                S.op(S.dve, lambda: nc.vector.tensor_tensor(out=x["cen"][:], in0=x["hh"][:, hs, :], in1=x["fmean"][:], op=ALU.subtract), reads=[B["hh"], B["fmean"]], writes=[B["cen"]])
                                yield
                                S.op(S.act, lambda: nc.scalar.activation(out=x["fsq"][:], in_=x["cen"][:], func=AF.Square), reads=[B["cen"]], writes=[B["fsq"]])
                                yield
                                S.op(S.pe, lambda: nc.tensor.matmul(NUMF, lhsT=ones32, rhs=fl(x["fsq"][:]), start=True, stop=True),
                                     reads=[B["fsq"], CONSTB], writes=[PSB[iNM]])
                                yield
                                S.op(S.act, lambda: nc.scalar.activation(out=fl(x["frt"][:]), in_=NUMF, func=AF.Sqrt, bias=EPS, scale=1.0 / 128), reads=[PSB[iNM]], writes=[B["frt"]])
                                yield
                                S.op(S.dve, lambda: nc.vector.reciprocal(out=x["frs"][:], in_=x["frt"][:]), reads=[B["frt"]], writes=[B["frs"]])
                                S.op(S.dve, lambda: nc.vector.tensor_tensor(out=x["cen"][:], in0=x["cen"][:], in1=x["frs"][:], op=ALU.mult), reads=[B["cen"], B["frs"]], writes=[B["cen"]])

                                def m16op(half=half):
                                    for hh_ in range(4):
                                        h = 4 * half + hh_
                                        ins = nc.vector.scalar_tensor_tensor(out=x["m16"][:, h, :], in0=x["cen"][:, hh_, :], scalar=VEC[l][:, V_MNW + h:V_MNW + h + 1], in1=x["SOc"][:, h, :], op0=ALU.mult, op1=ALU.mult)
                                    return ins
                                S.op(S.dve, m16op, reads=[B["cen"], B["SOc"], VECB], writes=[B["m16"]], partial=True)
                                yield
                            S.dma(S.sp, MIXT[:, 4:12, tok0:tok0 + 128], x["m16"][:], B["m16"], reads=[B["m16"]])
                        visited.add(ch)

                Fo = list(range(18))
                Bo = [1, 0] + list(range(17, 1, -1))
                loads(0, Fo[0], 0)
                loads(1, Bo[0], 0)
                for i in range(18):
                    par = i % 2
                    if i + 1 < 18:
                        loads(0, Fo[i + 1], 1 - par)
                        loads(1, Bo[i + 1], 1 - par)
                    gens = [unit(0, Fo[i], par, i == 0, i == 17), unit(1, Bo[i], par, i == 0, i == 17)]
                    while gens:
                        for g_ in list(gens):
                            try:
                                next(g_)
                            except StopIteration:
                                gens.remove(g_)
                S.barrier()
                rel = []
                for d in range(2):
                    for v in X[d]["B"].values():
                        rel += v if isinstance(v, list) else [v]
                S.release(rel)

        def phase_outffn(l):
            xsrc = xT if l == 0 else XS
            with contextlib.ExitStack() as st:
                Xb = sbt("Xb", [128, 16, 512], F32, st); XBB = S.buf("Xb", dma=True)
                S1 = sbt("S1", [128, 16, 512], F32, st); S1B = [S.buf(f"S1_{c}") for c in range(16)]
                A16 = sbt("A16", [128, 16, 512], BF16, st); A16B = S.buf("A16", dma=True)
                HID = sbt("HID", [128, 64, 512], BF16, st); HIDB = [S.buf(f"HID{c}") for c in range(64)]
                RL = [sbt(f"RL{i}", [128, 512], F32, st) for i in range(2)]; RLB = [S.buf(f"RL{i}") for i in range(2)]
                sq = [sbt(f"osq{i}", [128, 512], F32, st) for i in range(2)]; SQ = [S.buf(f"osq{i}") for i in range(2)]
                tmp = [sbt(f"otmp{i}", [128, 512], F32, st) for i in range(2)]; TMP = [S.buf(f"otmp{i}") for i in range(2)]
                rt = sbt("ort", [128, 512], F32, st); RT = S.buf("ort")
                rstd = sbt("orstd", [128, 512], F32, st); RSTD = S.buf("orstd")
                cnt = {"sq": 0}

                def stat_add(src_ap, src_bufs, n, c, ncs):
                    k = cnt["sq"] % 2; cnt["sq"] += 1
                    S.op(S.act, lambda: nc.scalar.activation(out=sq[k][:, :n], in_=src_ap, func=AF.Square), reads=src_bufs, writes=[SQ[k]])
                    S.op(S.pe, lambda: nc.tensor.matmul(PS[0][:, :n], lhsT=ones32, rhs=sq[k][:, :n], start=(c == 0), stop=(c == ncs - 1)),
                         reads=[SQ[k], CONSTB], writes=[PSB[0]], partial=(c > 0))

                def resid(G, w, n):
                    for c in range(16):
                        k = c % 2
                        S.op(S.dve, lambda: nc.vector.scalar_tensor_tensor(out=tmp[k][:, :n], in0=S1[:, c, :n], scalar=G[l][:, c, w:w + 1], in1=rstd[:, :n], op0=ALU.mult, op1=ALU.mult),
                             reads=[S1B[c], RSTD, DERB[l]], writes=[TMP[k]])
                        S.op(S.dve, lambda: nc.vector.tensor_tensor(out=Xb[:, c, :n], in0=Xb[:, c, :n], in1=tmp[k][:, :n], op=ALU.add),
                             reads=[TMP[k], XBB], writes=[XBB], partial=True)

                for bi, (t0, n) in enumerate(TBS):
                    if l == 1 and bi == 0:
                        continue
                    w = 0 if bi == 0 else 1
                    S.dma(S.sp, A16[:, :, :n], MIXT[:, :, t0:t0 + n], A16B, writes=[A16B])
                    S.dma(S.sp, Xb[:, :, :n], xsrc[:, :, t0:t0 + n], XBB, writes=[XBB])
                    for fb in range(16):
                        s = wget()
                        pb = nextps()
                        pv = PS[pb][:, :n]
                        S.op(S.pe, mm16(pv, s, lambda kc: A16[:, kc, :n]), reads=[RB[s], A16B], writes=[PSB[pb]])
                        S.op(S.act, lambda: nc.scalar.copy(out=S1[:, fb, :n], in_=pv), reads=[PSB[pb]], writes=[S1B[fb]])
                        stat_add(S1[:, fb, :n], [S1B[fb]], n, fb, 16)
                    rstd_from(0, n, rt, RT, rstd, RSTD, D)
                    resid(GG1, w, n)
                    def fin_store():
                        if l == L - 1:
                            S.dma(S.sp, outT[:, :, t0 - TC:t0 - TC + n], Xb[:, :, :n], XBB, reads=[XBB])
                        else:
                            S.dma(S.sp, XS[:, :, t0:t0 + n], Xb[:, :, :n], XBB, reads=[XBB])
                    if CUT == 8:
                        for _ in range(128):
                            wget()
                        fin_store(); continue
                    for c in range(16):
                        stat_add(Xb[:, c, :n], [XBB], n, c, 16)
                    rstd_from(0, n, rt, RT, rstd, RSTD, D)
                    for c in range(16):
                        k = c % 2
                        S.op(S.dve, lambda: nc.vector.scalar_tensor_tensor(out=tmp[k][:, :n], in0=Xb[:, c, :n], scalar=GSC2[l][:, c, w:w + 1], in1=rstd[:, :n], op0=ALU.mult, op1=ALU.mult),
                             reads=[XBB, RSTD, DERB[l]], writes=[TMP[k]])
                        S.op(S.act, lambda: nc.scalar.activation(out=A16[:, c, :n], in_=tmp[k][:, :n], func=AF.Identity, bias=MOD[l][:, 48 + c, w:w + 1], scale=1.0),
                             reads=[TMP[k], MODB[l]], writes=[A16B], partial=True)
                    if CUT == 9:
                        for _ in range(128):
                            wget()
                        fin_store(); continue
                    for hb in range(64):
                        s = wget()
                        pb = nextps()
                        pv = PS[pb][:, :n]
                        S.op(S.pe, mm16(pv, s, lambda kc: A16[:, kc, :n]), reads=[RB[s], A16B], writes=[PSB[pb]])
                        k = hb % 2
                        S.op(S.act, lambda: nc.scalar.activation(out=RL[k][:, :n], in_=pv, func=AF.Relu), reads=[PSB[pb]], writes=[RLB[k]])
                        S.op(S.dve, lambda: nc.vector.tensor_tensor(out=HID[:, hb, :n], in0=RL[k][:, :n], in1=RL[k][:, :n], op=ALU.mult), reads=[RLB[k]], writes=[HIDB[hb]])
                    if CUT == 10:
                        for _ in range(64):
                            wget()
                        fin_store(); continue
                    for fb in range(16):
                        pb = nextps()
                        pv = PS[pb][:, :n]
                        ss = wget(4)

                        def mm64(ss=ss, pv=pv):
                            for kg, s_ in enumerate(ss):
                                for kc in range(16):
                                    ins = nc.tensor.matmul(pv, lhsT=ring[s_][:, kc, :], rhs=HID[:, kg * 16 + kc, :n],
                                                           start=(kg == 0 and kc == 0), stop=(kg == 3 and kc == 15))
                            return ins
                        S.op(S.pe, mm64, reads=[RB[s_] for s_ in ss] + HIDB, writes=[PSB[pb]])
                        S.op(S.act, lambda: nc.scalar.copy(out=S1[:, fb, :n], in_=pv), reads=[PSB[pb]], writes=[S1B[fb]])
                        stat_add(S1[:, fb, :n], [S1B[fb]], n, fb, 16)
                    rstd_from(0, n, rt, RT, rstd, RSTD, D)
                    resid(GG2, w, n)
                    if l == L - 1:
                        S.dma(S.sp, outT[:, :, t0 - TC:t0 - TC + n], Xb[:, :, :n], XBB, reads=[XBB])
                    else:
                        S.dma(S.sp, XS[:, :, t0:t0 + n], Xb[:, :, :n], XBB, reads=[XBB])
                S.barrier()
                S.release([XBB, A16B])

        phase_adaln()
        for l in range(L):
            if stop_after == ("adaln", l):
                break
            phase_inproj(l)
            if stop_after == ("inproj", l):
                break
            phase_conv(l)
            if stop_after == ("conv", l):
                break
            phase_mlstm(l)
            if stop_after == ("mlstm", l):
                break
            phase_outffn(l)
            if stop_after == ("outffn", l):
                break
        S.barrier()
    return nc


def _tiles(w, tile_cols):
    K, N = w.shape
    return np.ascontiguousarray(w.reshape(K // 128, 128, N // tile_cols, tile_cols).transpose(2, 1, 0, 3))


def _col(v):
    return np.ascontiguousarray(v.reshape(-1, 128).T)


def prep_shared(inp):
    f = np.float32
    sh = {}
    sh["wada"] = np.stack([_tiles(inp["w_ada"][l], 128) for l in range(L)])
    sp = [0, 512, 1024, 2048, 3072, 4096, 5120, 5152, 5664, 6176, 6688]
    winf, wint, wing, wout, wff1, wff2, vecs = [], [], [], [], [], [], []
    for l in range(L):
        w = inp["w_in"][l]
        seg = lambda i: w[:, sp[i]:sp[i + 1]]
        a_val, a_gate, q, k, v, o, g, s_in, s_b, s_c = [seg(i) for i in range(10)]
        cols = []
        for i in range(4):
            cols += [a_gate[:, i * 128:(i + 1) * 128], a_val[:, i * 128:(i + 1) * 128]]
        cols += [q, k, o]
        for i in range(4):
            cols += [s_in[:, i * 128:(i + 1) * 128], s_c[:, i * 128:(i + 1) * 128]]
        cols += [s_b]
        winf.append(_tiles(np.concatenate(cols, axis=1), 128))
        wint.append(_tiles(np.concatenate([v, k], axis=1), 512))
        wing.append(_tiles(g, 32)[0])
        wout.append(_tiles(inp["w_out"][l], 128))
        wff1.append(_tiles(inp["w_ff1"][l], 128))
        w2 = inp["w_ff2"][l]
        t2 = w2.reshape(4, 16, 128, 16, 128).transpose(3, 0, 2, 1, 4).reshape(64, 128, 16, 128)
        wff2.append(np.ascontiguousarray(t2))
        vv = np.zeros((128, NV), f)
        vv[:, V_GPM:V_GPM + 16] = _col(inp["g_pre_mix"][l])
        vv[:, V_GQM:V_GQM + 16] = _col(inp["g_post_mix"][l])
        vv[:, V_GPF:V_GPF + 16] = _col(inp["g_pre_ffn"][l])
        vv[:, V_GQF:V_GQF + 16] = _col(inp["g_post_ffn"][l])
        vv[:, V_BADA:V_BADA + 96] = _col(inp["b_ada"][l])
        caw = inp["conv_a_w"][l]
        vv[:, V_CAW:V_CAW + 124] = caw.reshape(31, 4, 128).transpose(2, 1, 0).reshape(128, 124)
        vv[:, V_CAB:V_CAB + 4] = _col(inp["conv_a_b"][l])
        vv[:, V_LNW:V_LNW + 4] = _col(inp["ln_a_w"][l])
        vv[:, V_LNB:V_LNB + 4] = _col(inp["ln_a_b"][l])
        vv[:, V_MNW:V_MNW + 8] = _col(inp["mlstm_norm_w"][l])
        ccw = inp["conv_c_w"][l]
        vv[:, V_CCW:V_CCW + 12] = ccw.reshape(3, 4, 128).transpose(2, 1, 0).reshape(128, 12)
        vv[:, V_BG:V_BG + 32] = np.broadcast_to(inp["b_gates"][l][None, :], (128, 32))
        vecs.append(vv)
    sh["winf"] = np.stack(winf); sh["wint"] = np.stack(wint); sh["wing"] = np.stack(wing)
    sh["wout"] = np.stack(wout); sh["wff1"] = np.stack(wff1); sh["wff2"] = np.stack(wff2)
    sh["vecs"] = np.stack(vecs)
    c = np.zeros((128, 768), f)
    c[:, C_ID:C_ID + 128] = np.eye(128, dtype=f)
    c[:, C_ONE:C_ONE + 128] = 1.0
    ii = np.arange(128)
    mf = (ii[:, None] <= ii[None, :]).astype(f)
    mb = (ii[:, None] >= ii[None, :]).astype(f)
    c[:, C_MF:C_MF + 128] = mf
    c[:, C_MB:C_MB + 128] = mb
    c[:, C_NF:C_NF + 128] = (mf - 1.0) * 30000.0
    c[:, C_NB:C_NB + 128] = (mb - 1.0) * 30000.0
    sh["consts"] = c
    return sh


def prep_core(inp, b):
    X = np.concatenate([inp["ctx"][b], inp["x"][b]], axis=0)
    xT = np.ascontiguousarray(X.T.reshape(16, 128, T).transpose(1, 0, 2))
    cvv = np.stack([_col(inp["c_ctx"]), _col(inp["c"][b])], axis=-1)
    return {"xT": xT, "cv": np.ascontiguousarray(cvv.astype(np.float32))}


_NC_CACHE = {}


def kernel(**inputs):
    inp = {k: np.asarray(v, dtype=np.float32) for k, v in inputs.items()}
    sh = prep_shared(inp)
    in_maps = []
    for b in range(8):
        m = dict(sh)
        m.update(prep_core(inp, b))
        in_maps.append(m)
    if "nc" not in _NC_CACHE:
        _NC_CACHE["nc"] = build()
    res = run_bass_kernel_spmd(_NC_CACHE["nc"], in_maps, core_ids=list(range(8)))
    out = np.empty((8, 2048, 2048), np.float32)
    for b in range(8):
        oT = res.results[b]["outT"]
        out[b] = oT.transpose(2, 1, 0).reshape(2048, 2048)
    return out
```
